# Optimizing a Trainium2 kernel written in Bass

```python
import jax, jax.numpy as jnp
from jax import lax
import numpy as np

D_MODEL = 1024
BATCH = 4
SEQ = 8192
DEPTH = 2

CHUNK = 64
D_MIX = D_MODEL
D_A = D_MIX // 2
D_B = D_MIX - D_A
A_HEADS = 4
A_HEAD_DIM = D_A // A_HEADS
SGU_BLOCK = 128
B_HEADS = 4
B_V_DIM = D_B // B_HEADS
B_QK_DIM = B_V_DIM // 2
D_QK = B_HEADS * B_QK_DIM
QK_CONV = 4
D_FF = 2816
FFN_CONV = 3
LN_EPS = 1e-5
DEEPNORM_ALPHA = (2 * DEPTH) ** 0.25
DEEPNORM_BETA = (8 * DEPTH) ** -0.25
D_IN = 2 * D_A + 2 * D_QK + 2 * D_B + 2 * B_HEADS
SPLITS = [D_A, 2 * D_A, 2 * D_A + 2 * D_QK, 2 * D_A + 2 * D_QK + D_B,
          2 * D_A + 2 * D_QK + 2 * D_B, 2 * D_A + 2 * D_QK + 2 * D_B + B_HEADS]

kernel_name = 'hybrid_sgu_mlstm_deepnorm_encoder'


def layer_norm(x, g, b=None):
    xf = x.astype(jnp.float32)
    mu = xf.mean(-1, keepdims=True)
    var = jnp.square(xf - mu).mean(-1, keepdims=True)
    y = (xf - mu) * lax.rsqrt(var + LN_EPS) * g.astype(jnp.float32)
    if b is not None:
        y = y + b.astype(jnp.float32)
    return y.astype(x.dtype)


def causal_dwconv(x, w, b):
    width = w.shape[0]
    s_len = x.shape[1]
    xp = jnp.pad(x, ((0, 0), (width - 1, 0), (0, 0)))
    y = b
    for j in range(width):
        y = y + xp[:, j:j + s_len] * w[j]
    return y


def spatial_gating(u, v, w_s, b_s, ln_g, ln_b):
    bsz, s_len = u.shape[:2]
    nb = s_len // SGU_BLOCK
    v = layer_norm(v, ln_g, ln_b)
    chunk_id = jnp.arange(SGU_BLOCK) // CHUNK
    mask = chunk_id[:, None] >= chunk_id[None, :]
    w = jnp.where(mask, w_s, jnp.zeros_like(w_s))
    vb = v.reshape(bsz, nb, SGU_BLOCK, A_HEADS, A_HEAD_DIM)
    s = jnp.einsum('hij,bnjhc->bnihc', w, vb) + b_s.T[None, None, :, :, None]
    return u * s.reshape(bsz, s_len, A_HEADS, A_HEAD_DIM)


def mlstm_chunkwise(q, k, v, i_pre, log_f):
    bsz, nh, s_len, dk = q.shape
    dv = v.shape[-1]
    nc = s_len // CHUNK
    f32 = jnp.float32
    q = q.astype(f32).reshape(bsz, nh, nc, CHUNK, dk)
    k = (k.astype(f32) * dk ** -0.5).reshape(bsz, nh, nc, CHUNK, dk)
    v = v.astype(f32).reshape(bsz, nh, nc, CHUNK, dv)
    ig = i_pre.astype(f32).reshape(bsz, nh, nc, CHUNK)
    lf = log_f.astype(f32).reshape(bsz, nh, nc, CHUNK)
    b = jnp.cumsum(lf, axis=-1)
    b_end = b[..., -1]
    g = b_end[..., None] - b + ig

    def step(carry, xs):
        c, n, m = carry
        a, g_c, k_c, v_c = xs
        m_new = jnp.maximum(a + m, g_c.max(-1))
        decay = jnp.exp(a + m - m_new)
        kw = k_c * jnp.exp(g_c - m_new[..., None])[..., None]
        c_new = decay[..., None, None] * c + jnp.einsum('bhlk,bhlv->bhkv', kw, v_c)
        n_new = decay[..., None] * n + kw.sum(-2)
        return (c_new, n_new, m_new), (c, n, m)

    init = (jnp.zeros((bsz, nh, dk, dv), f32), jnp.zeros((bsz, nh, dk), f32), jnp.zeros((bsz, nh), f32))
    xs = (jnp.moveaxis(b_end, 2, 0), jnp.moveaxis(g, 2, 0), jnp.moveaxis(k, 2, 0), jnp.moveaxis(v, 2, 0))
    _, (c_prev, n_prev, m_prev) = lax.scan(step, init, xs)
    c_prev = jnp.moveaxis(c_prev, 0, 2)
    n_prev = jnp.moveaxis(n_prev, 0, 2)
    m_prev = jnp.moveaxis(m_prev, 0, 2)

    causal = jnp.tril(jnp.ones((CHUNK, CHUNK), dtype=bool))
    log_d = jnp.where(causal, b[..., :, None] - b[..., None, :] + ig[..., None, :], -jnp.inf)
    log_inter = b + m_prev[..., None]
    m = jnp.maximum(log_d.max(-1), log_inter)
    w_intra = jnp.exp(log_d - m[..., None])
    w_inter = jnp.exp(log_inter - m)
    scores = jnp.einsum('bhntk,bhnsk->bhnts', q, k) * w_intra
    num = (jnp.einsum('bhnts,bhnsv->bhntv', scores, v)
           + w_inter[..., None] * jnp.einsum('bhntk,bhnkv->bhntv', q, c_prev))
    den = scores.sum(-1) + w_inter * jnp.einsum('bhntk,bhnk->bhnt', q, n_prev)
    h = num / jnp.maximum(jnp.abs(den), jnp.exp(-m))[..., None]
    return h.reshape(bsz, nh, s_len, dv)


def hybrid_mixer(x, w_in, b_igate, b_fgate, qk_conv_w, qk_conv_b, sgu_w, sgu_b, sgu_ln_g, sgu_ln_b,
                 mh_norm_g, w_out):
    bsz, s_len, _ = x.shape
    z = x @ w_in
    a_u, a_v, qk, b_v, b_o, g_i, g_f = jnp.split(z, SPLITS, axis=-1)
    a_u = jax.nn.gelu(a_u, approximate=False).reshape(bsz, s_len, A_HEADS, A_HEAD_DIM)
    a_v = jax.nn.gelu(a_v, approximate=False).reshape(bsz, s_len, A_HEADS, A_HEAD_DIM)
    y_a = spatial_gating(a_u, a_v, sgu_w, sgu_b, sgu_ln_g, sgu_ln_b).reshape(bsz, s_len, D_A)
    qk = jax.nn.silu(causal_dwconv(qk, qk_conv_w, qk_conv_b))
    q, k = jnp.split(qk, 2, axis=-1)
    q = q.reshape(bsz, s_len, B_HEADS, B_QK_DIM).transpose(0, 2, 1, 3)
    k = k.reshape(bsz, s_len, B_HEADS, B_QK_DIM).transpose(0, 2, 1, 3)
    v = b_v.reshape(bsz, s_len, B_HEADS, B_V_DIM).transpose(0, 2, 1, 3)
    i_pre = (g_i + b_igate).transpose(0, 2, 1)
    log_f = jax.nn.log_sigmoid((g_f + b_fgate).astype(jnp.float32)).transpose(0, 2, 1)
    h = mlstm_chunkwise(q, k, v, i_pre, log_f).transpose(0, 2, 1, 3)
    h = layer_norm(h, mh_norm_g).astype(x.dtype)
    o = jax.nn.sigmoid(b_o).reshape(bsz, s_len, B_HEADS, B_V_DIM)
    y_b = (o * h).reshape(bsz, s_len, D_B)
    return jnp.concatenate([y_a, y_b], axis=-1) @ w_out


def conv_glu_ffn(x, w_up, conv_w, conv_b, w_down):
    gate, up = jnp.split(x @ w_up, 2, axis=-1)
    gate = jax.nn.gelu(causal_dwconv(gate, conv_w, conv_b), approximate=False)
    return (gate * up) @ w_down


def setup_inputs(seed: int = 0) -> dict:
    key = jax.random.key(seed)
    ks = jax.random.split(key, 20)

    def nrm(k, shape, scale):
        return jax.random.normal(k, shape, jnp.float32) * scale

    L = DEPTH
    return {
        'x': nrm(ks[0], (BATCH, SEQ, D_MODEL), 1.0),
        'w_in': nrm(ks[1], (L, D_MODEL, D_IN), D_MODEL ** -0.5),
        'b_igate': nrm(ks[2], (L, B_HEADS), 0.1),
        'b_fgate': jnp.linspace(3.0, 6.0, B_HEADS, dtype=jnp.float32)[None] + nrm(ks[3], (L, B_HEADS), 0.01),
        'qk_conv_w': nrm(ks[4], (L, QK_CONV, 2 * D_QK), QK_CONV ** -0.5),
        'qk_conv_b': nrm(ks[5], (L, 2 * D_QK), 0.02),
        'sgu_w': nrm(ks[6], (L, A_HEADS, SGU_BLOCK, SGU_BLOCK), SGU_BLOCK ** -0.5),
        'sgu_b': 1.0 + nrm(ks[7], (L, A_HEADS, SGU_BLOCK), 0.02),
        'sgu_ln_g': 1.0 + nrm(ks[8], (L, A_HEADS, A_HEAD_DIM), 0.02),
        'sgu_ln_b': nrm(ks[9], (L, A_HEADS, A_HEAD_DIM), 0.02),
        'mh_norm_g': 1.0 + nrm(ks[10], (L, B_HEADS, B_V_DIM), 0.02),
        'w_out': nrm(ks[11], (L, D_MIX, D_MODEL), DEEPNORM_BETA * D_MIX ** -0.5),
        'ln1_g': 1.0 + nrm(ks[12], (L, D_MODEL), 0.02),
        'ln1_b': nrm(ks[13], (L, D_MODEL), 0.02),
        'ffn_w_up': nrm(ks[14], (L, D_MODEL, 2 * D_FF), D_MODEL ** -0.5),
        'ffn_conv_w': nrm(ks[15], (L, FFN_CONV, D_FF), FFN_CONV ** -0.5),
        'ffn_conv_b': nrm(ks[16], (L, D_FF), 0.02),
        'ffn_w_down': nrm(ks[17], (L, D_FF, D_MODEL), DEEPNORM_BETA * D_FF ** -0.5),
        'ln2_g': 1.0 + nrm(ks[18], (L, D_MODEL), 0.02),
        'ln2_b': nrm(ks[19], (L, D_MODEL), 0.02),
    }


def reference(x, w_in, b_igate, b_fgate, qk_conv_w, qk_conv_b, sgu_w, sgu_b, sgu_ln_g, sgu_ln_b,
              mh_norm_g, w_out, ln1_g, ln1_b, ffn_w_up, ffn_conv_w, ffn_conv_b, ffn_w_down,
              ln2_g, ln2_b):
    for l in range(DEPTH):
        mix = hybrid_mixer(x, w_in[l], b_igate[l], b_fgate[l], qk_conv_w[l], qk_conv_b[l], sgu_w[l],
                           sgu_b[l], sgu_ln_g[l], sgu_ln_b[l], mh_norm_g[l], w_out[l])
        x = layer_norm(DEEPNORM_ALPHA * x + mix, ln1_g[l], ln1_b[l])
        ffn = conv_glu_ffn(x, ffn_w_up[l], ffn_conv_w[l], ffn_conv_b[l], ffn_w_down[l])
        x = layer_norm(DEEPNORM_ALPHA * x + ffn, ln2_g[l], ln2_b[l])
    return x
```

```python
import numpy as np
import concourse.bass as bass
import concourse.mybir as mybir
from concourse.bass_utils import run_bass_kernel_spmd
from contextlib import ExitStack

import os
LIMIT = int(os.environ.get("KLIMIT", "1000000000"))
ABL = os.environ.get("KABL", "")
F32 = mybir.dt.float32
BF16 = mybir.dt.bfloat16
AF = mybir.ActivationFunctionType
ALU = mybir.AluOpType
AX = mybir.AxisListType


class Tile:
    def __init__(self, name, ap):
        self.name = name
        self.ap = ap
        self.w = None
        self.r = []

    def view(self, name, ap):
        t = Tile.__new__(Tile)
        t.name = name
        t.ap = ap
        t._parent = self
        return t


def _root(t):
    while hasattr(t, "_parent"):
        t = t._parent
    return t


class Prog:
    ENG = ("pe", "act", "dve", "pool", "sp")
    NDMA = 12

    def __init__(self, nc, es):
        self.nc = nc
        self.es = es
        self.items = {e: [] for e in self.ENG}
        self.sems = {}
        self.count = {}
        for e in ("pe", "act", "dve", "pool"):
            self.sems[e] = es.enter_context(nc.semaphore("s_" + e))
            self.count[e] = 0
        self.dma_sems = {}
        self.dma_rr = {}
        for q in ("sp", "pool", "act"):
            for i in range(self.NDMA):
                k = "d_%s_%d" % (q, i)
                self.sems[k] = es.enter_context(nc.semaphore(k))
                self.count[k] = 0
            self.dma_rr[q] = 0
        self.waited = {e: {} for e in self.ENG}
        self.out_events = []
        self.n_ops = 0
        self.role_ap = None

    def sbuf(self, name, shape, dtype, es=None, inherit=None):
        t = (es or self.es).enter_context(self.nc.sbuf_tensor(name, list(shape), dtype))
        tl = Tile(name, t[:])
        if inherit:
            tl.r = list(inherit)
        return tl

    def psum(self, name, shape, dtype):
        t = self.es.enter_context(self.nc.psum_tensor(name, list(shape), dtype))
        return Tile(name, t[:])

    def _need(self, eng, ev):
        if ev is None:
            return
        k, v = ev
        if self.waited[eng].get(k, 0) >= v:
            return
        self.waited[eng][k] = v
        self.items[eng].append(("wait", k, v))

    def _deps(self, eng, reads, writes):
        for t in reads:
            t = _root(t)
            self._need(eng, t.w)
        for t in writes:
            t = _root(t)
            self._need(eng, t.w)
            for ev in t.r:
                self._need(eng, ev)

    def _mark(self, ev, reads, writes):
        for t in reads:
            t = _root(t)
            t.r.append(ev)
        for t in writes:
            t = _root(t)
            t.w = ev
            t.r = []

    def op(self, eng, fn, reads=(), writes=()):
        self.n_rec = getattr(self, "n_rec", 0) + 1
        if self.n_rec > LIMIT:
            return None
        self._deps(eng, reads, writes)
        self.count[eng] += 1
        ev = (eng, self.count[eng])
        self.items[eng].append(("op", fn, eng, 1))
        self.waited[eng][eng] = max(self.waited[eng].get(eng, 0), 0)
        self._mark(ev, reads, writes)
        self.n_ops += 1
        return ev

    def dram(self, name, shape, dtype):
        t = self.nc.dram_tensor(name, list(shape), dtype, kind="Internal")
        return Tile(name, t.ap())

    def collective(self, send, recv, groups):
        return self.dma("pool", recv, send, None, coll=groups)

    def dma(self, q, dst, src, fn, out_dram=False, cond=None, coll=None):
        self.n_rec = getattr(self, "n_rec", 0) + 1
        if self.n_rec > LIMIT:
            return None
        reads = [src] if src is not None else []
        writes = [dst] if dst is not None else []
        eng = q
        self._deps(eng, reads, writes)
        i = self.dma_rr[q]
        self.dma_rr[q] = (i + 1) % self.NDMA
        k = "d_%s_%d" % (q, i)
        if self.count[k] > 0:
            self._need(eng, (k, self.count[k]))
        self.count[k] += 16
        ev = (k, self.count[k])
        if coll is not None:
            self.items[eng].append(("op", lambda e, dst=dst, src=src, coll=coll: e.collective_compute(
                "AllGather", ALU.bypass, replica_groups=coll, ins=[src.ap], outs=[dst.ap]), k, 16))
        elif cond is None:
            self.items[eng].append(("op", lambda e, fn=fn: e.dma_start(out=fn()[0], in_=fn()[1]), k, 16))
        else:
            self.items[eng].append(("op", lambda e, fn=fn, cond=cond: e.dma_start(out=fn()[0], in_=fn()[1], cond=cond(self.rt[e])), k, 16))
        self._mark(ev, reads, writes)
        if out_dram:
            self.out_events.append(ev)
        return ev

    def make_identity(self, ident, eng="pool"):
        n = ident.ap.shape[0]
        self.op("pool", lambda e: e.memset(ident.ap, 0.0), writes=[ident])
        self.op("pool", lambda e: e.affine_select(
            out=ident.ap, in_=ident.ap, pattern=[[-1, n]], compare_op=ALU.not_equal,
            fill=1.0, base=0, channel_multiplier=1), reads=[ident], writes=[ident])

    def finish(self):
        for ev in self.out_events:
            self._need("sp", ev)
        nc = self.nc
        items = self.items
        sems = self.sems

        self.rt = {}

        def replay(eng_name, e):
            if self.role_ap is not None and eng_name in ("sp", "pool"):
                self.rt[e] = e.value_load(self.role_ap, min_val=0, max_val=1)
            for it in items[eng_name]:
                if it[0] == "wait":
                    e.wait_ge(sems[it[1]], it[2])
                else:
                    ins = it[1](e)
                    ins.then_inc(sems[it[2]], it[3])

        with nc.Block() as block:
            @block.tensor
            def _(e):
                replay("pe", e)

            @block.scalar
            def _(e):
                replay("act", e)

            @block.vector
            def _(e):
                replay("dve", e)

            @block.gpsimd
            def _(e):
                replay("pool", e)

            @block.sync
            def _(e):
                replay("sp", e)


L = 2
D = 1024
DIN = 2568
DFF = 2816
NJ = 22
ST = 1024
NB = ST // 128
HT = 512
ALPHA = float((2 * L) ** 0.25)
LN_EPS = 1e-5
LN8 = float(np.log(8.0))
NPP = 116
NRING = 8


class WRing:
    def __init__(self, P, n):
        self.P = P
        self.n = n
        self.slots = [P.sbuf("wr%d" % i, [128, 4096], BF16) for i in range(n)]
        self.busy = [False] * n
        self.pending = []
        self.loaded = {}
        self.seq = 0

    def request(self, key, mk):
        self.pending.append((key, mk))

    def pump(self):
        while self.pending:
            s = self.seq % self.n
            if self.busy[s]:
                return
            key, mk = self.pending.pop(0)
            slot = self.slots[s]
            self.P.dma("pool", slot, None, lambda mk=mk, slot=slot: mk(slot.ap))
            self.busy[s] = True
            self.loaded[key] = s
            self.seq += 1

    def get(self, key):
        self.pump()
        assert key in self.loaded, key
        return self.slots[self.loaded[key]]

    def release(self, key):
        s = self.loaded.pop(key)
        self.busy[s] = False
        self.pump()


def build_program(n_tok):
    nst = n_tok // ST
    nc = bass.Bass("TRN2", target_bir_lowering=False)
    dt = lambda name, shape, kind="ExternalInput": nc.dram_tensor(name, list(shape), F32, kind=kind).ap()
    x_d = dt("x", [n_tok, D])
    w_in_d = dt("w_in", [L, D, DIN])
    w_out_d = dt("w_out", [L, D, D])
    w_up_d = dt("w_up", [L, D, 2 * DFF])
    w_dn_d = dt("w_dn", [L, DFF, D])
    pp_d = dt("pp", [L, 128, NPP])
    sguT_d = dt("sguT", [L, 128, 4, 128])
    sgub_d = dt("sgub", [L, 512])
    lnv_d = dt("lnv", [L, 5, D])
    gb_d = dt("gb", [L, 4, 2])
    y_d = dt("y", [n_tok, D], kind="ExternalOutput")

    with ExitStack() as es:
        P = Prog(nc, es)
        S = lambda name, shape, dtype=F32: P.sbuf(name, shape, dtype)
        ident = S("ident", [128, 128])
        cmask = S("cmask", [128, 128])
        ones128 = S("ones128", [128, 128])
        sel = S("sel", [4, 2, 128])
        selh = S("selh", [4, 4, 128])
        bdmask = S("bdmask", [128, 258])
        scanm = S("scanm", [4, HT])
        epst = S("epst", [128, 1])
        xs = [S("xs%d" % b, [128, D]) for b in range(NB)]
        ring = WRing(P, NRING)
        wg = S("wg", [128, 8, 8], BF16)
        PPt = S("PPt", [128, NPP])
        MHG = S("MHG", [128, 512])
        gbt = S("gbt", [4, 2])
        nbf = S("nbf", [4, 1])
        wsT = S("wsT", [128, 4, 128], BF16)
        Kh = S("Kh", [128, 4, 128])
        Cst = [[S("C%d_%d" % (l, pc), [128, 258]) for pc in range(2)] for l in range(L)]
        mst = [S("m%d" % l, [4, 1]) for l in range(L)]
        qkh = [[S("qkh%d_%d" % (l, c), [128, 3]) for c in range(4)] for l in range(L)]
        ghalo = [S("ghalo%d" % l, [128, NJ, 2]) for l in range(L)]
        banks = [P.psum("bank%d" % i, [128, 512], F32) for i in range(8)]
        bstate = {"i": 0}

        held = set()

        def bank(hold=False):
            while (bstate["i"] % 8) in held:
                bstate["i"] += 1
            k = bstate["i"] % 8
            bstate["i"] += 1
            if hold:
                held.add(k)
            return banks[k]

        def unhold(b):
            held.discard(banks.index(b))

        P.make_identity(ident)
        P.op("pool", lambda e: e.memset(cmask.ap, 1.0), writes=[cmask])
        P.op("pool", lambda e: e.affine_select(out=cmask.ap, in_=cmask.ap, pattern=[[1, 128]],
                                               compare_op=ALU.is_ge, fill=0.0, base=0, channel_multiplier=-1),
             reads=[cmask], writes=[cmask])
        P.op("pool", lambda e: e.memset(ones128.ap, 1.0), writes=[ones128])
        P.op("pool", lambda e: e.memset(sel.ap, 1.0), writes=[sel])
        for pc in range(2):
            P.op("pool", lambda e, pc=pc: e.affine_select(out=sel.ap[:, pc, :], in_=sel.ap[:, pc, :], pattern=[[1, 128]],
                                                          compare_op=ALU.is_ge, fill=0.0, base=128 * pc, channel_multiplier=-64),
                 reads=[sel], writes=[sel])
            P.op("pool", lambda e, pc=pc: e.affine_select(out=sel.ap[:, pc, :], in_=sel.ap[:, pc, :], pattern=[[-1, 128]],
                                                          compare_op=ALU.is_ge, fill=0.0, base=63 - 128 * pc, channel_multiplier=64),
                 reads=[sel], writes=[sel])
        for hh in range(4):
            P.op("pool", lambda e, hh=hh: e.tensor_copy(selh.ap[:, hh, :], sel.ap[:, hh // 2, :]), reads=[sel], writes=[selh])
            z0 = 64 if hh % 2 == 0 else 0
            P.op("pool", lambda e, hh=hh, z0=z0: e.memset(selh.ap[:, hh, z0:z0 + 64], 0.0), writes=[selh])
        P.op("pool", lambda e: e.memset(bdmask.ap, 0.0), writes=[bdmask])
        P.op("pool", lambda e: e.memset(bdmask.ap[0:64, 0:129], 1.0), writes=[bdmask])
        P.op("pool", lambda e: e.memset(bdmask.ap[64:128, 129:258], 1.0), writes=[bdmask])
        P.op("pool", lambda e: e.memset(scanm.ap, 1.0), writes=[scanm])
        P.op("pool", lambda e: e.memset(epst.ap, LN_EPS), writes=[epst])
        P.op("pool", lambda e: e.memset(scanm.ap.rearrange("p (c t) -> p c t", t=128)[:, :, 0:1], 0.0), writes=[scanm])
        for l in range(L):
            for pc in range(2):
                P.op("pool", lambda e, t=Cst[l][pc]: e.memset(t.ap, 0.0), writes=[Cst[l][pc]])
            P.op("pool", lambda e, t=mst[l]: e.memset(t.ap, 0.0), writes=[mst[l]])
            for c in range(4):
                P.op("pool", lambda e, t=qkh[l][c]: e.memset(t.ap, 0.0), writes=[qkh[l][c]])
            P.op("pool", lambda e, t=ghalo[l]: e.memset(t.ap, 0.0), writes=[ghalo[l]])

        def kview(w_d, c0, n):
            return w_d.rearrange("(kc p) n -> p kc n", p=128)[:, :, c0:c0 + n]

        def request_stage_weights(l):
            for p in range(5):
                ring.request(("in", p), lambda slot, p=p, l=l: (
                    slot[:, 0:4096].rearrange("p (kc n) -> p kc n", kc=8), kview(w_in_d[l], 512 * p, 512)))
            for h in range(2):
                ring.request(("out", h), lambda slot, h=h, l=l: (
                    slot[:, 0:4096].rearrange("p (kc n) -> p kc n", kc=8), kview(w_out_d[l], 512 * h, 512)))
            for J in range(6):
                n = 512 if J < 5 else 256
                ring.request(("upg", J), lambda slot, J=J, n=n, l=l: (
                    slot[:, 0:8 * n].rearrange("p (kc n) -> p kc n", kc=8), kview(w_up_d[l], 512 * J, n)))
                ring.request(("upu", J), lambda slot, J=J, n=n, l=l: (
                    slot[:, 0:8 * n].rearrange("p (kc n) -> p kc n", kc=8), kview(w_up_d[l], DFF + 512 * J, n)))
            for G in range(6):
                nj = 4 if G < 5 else 2
                ring.request(("dn", G), lambda slot, G=G, nj=nj, l=l: (
                    slot[:, 0:nj * 1024].rearrange("p (jc n) -> p jc n", jc=nj),
                    w_dn_d[l].rearrange("(jc p) n -> p jc n", p=128)[:, 4 * G:4 * G + nj, :]))

        prev_events = []

        def collect(tiles):
            evs = []
            for t in tiles:
                if t.w is not None:
                    evs.append(t.w)
                evs.extend(t.r)
            return list(dict.fromkeys(evs))

        def ln_rows(eng_stats, src_ap, nchunk, width, mv, st6, rs, tiles_r, tiles_w, name):
            for i in range(nchunk):
                P.op("dve", lambda e, i=i: e.bn_stats(st6.ap[:, i, :], src_ap[:, i * width:(i + 1) * width]),
                     reads=tiles_r, writes=[st6])
            P.op("dve", lambda e: e.bn_aggr(mv.ap, st6.ap[:, 0:nchunk, :].rearrange("p a b -> p (a b)")),
                 reads=[st6], writes=[mv])
            P.op("act", lambda e: e.activation(rs.ap, mv.ap[:, 1:2], AF.Sqrt, bias=epst.ap[:, 0:1]), reads=[mv, epst], writes=[rs])
            P.op("dve", lambda e: e.reciprocal(rs.ap, rs.ap), reads=[rs], writes=[rs])

        def stage(st, l):
            nonlocal prev_events
            if True:
                sid = "s%d_%d" % (st, l)
                request_stage_weights(l)
                ring.pump()
                P.dma("sp", PPt, None, lambda l=l: (PPt.ap, pp_d[l]))
                P.dma("sp", MHG, None, lambda l=l: (MHG.ap, lnv_d[l, 4, 0:512].partition_broadcast(128)))
                P.dma("sp", gbt, None, lambda l=l: (gbt.ap, gb_d[l]))
                P.dma("pool", wg, None, lambda l=l: (wg.ap, kview(w_in_d[l], 2560, 8)))
                P.dma("sp", Kh, None, lambda l=l: (Kh.ap.rearrange("p h i -> p (h i)"), sgub_d[l].partition_broadcast(128)))
                P.op("dve", lambda e: e.tensor_scalar(nbf.ap, gbt.ap[:, 1:2], -1.0, None, ALU.mult), reads=[gbt], writes=[nbf])
                if l == 0 and st == 0:
                    for b in range(NB):
                        P.dma("sp", xs[b], None, lambda st=st, b=b: (xs[b].ap, x_d[st * ST + b * 128:st * ST + (b + 1) * 128, :]))

                with ExitStack() as mes:
                    inh = prev_events
                    M = lambda name, shape, dtype=F32: P.sbuf(sid + name, shape, dtype, es=mes, inherit=inh)
                    mt = []

                    def MT(name, shape, dtype=F32):
                        t = M(name, shape, dtype)
                        mt.append(t)
                        return t
                    G1 = MT("G1", [128, D]); B1 = MT("B1", [128, D])
                    P.dma("sp", G1, None, lambda l=l: (G1.ap, lnv_d[l, 0, :].partition_broadcast(128)))
                    P.dma("sp", B1, None, lambda l=l: (B1.ap, lnv_d[l, 1, :].partition_broadcast(128)))
                    wsT32 = MT("wsT32", [128, 4, 128])
                    P.dma("sp", wsT32, None, lambda l=l: (wsT32.ap, sguT_d[l]))
                    P.op("dve", lambda e: e.memset(wsT32.ap[64:128, :, 0:64], 0.0), writes=[wsT32])
                    P.op("act", lambda e: e.copy(wsT.ap, wsT32.ap), reads=[wsT32], writes=[wsT])
                    bk = bank()
                    P.op("pe", lambda e, bk=bk: e.matmul(bk.ap, ones128.ap, wsT32.ap.rearrange("p h i -> p (h i)"), start=True, stop=True),
                         reads=[ones128, wsT32], writes=[bk])
                    for hh in range(4):
                        P.op("dve", lambda e, hh=hh, bk=bk: e.scalar_tensor_tensor(
                            Kh.ap[:, hh, :], bk.ap[:, hh * 128:(hh + 1) * 128], PPt.ap[:, 112 + hh:113 + hh], Kh.ap[:, hh, :],
                            ALU.mult, ALU.add), reads=[bk, PPt, Kh], writes=[Kh])

                    xT = MT("xT", [128, 8, HT], BF16)
                    uT = MT("uT", [128, 4, HT])
                    vtmp = [MT("vtmp%d" % i, [128, 512]) for i in range(4)]
                    vmvA = MT("vmvA", [128, 4, 4, 2]); vrsA = MT("vrsA", [128, 4, 4])
                    vmv = [Tile.view(vmvA, "vmv%d" % i, vmvA.ap[:, i]) for i in range(4)]
                    vrs = [Tile.view(vrsA, "vrs%d" % i, vrsA.ap[:, i]) for i in range(4)]
                    mvB = MT("mvB", [128, 4, 2]); rsB = MT("rsB", [128, 4])
                    vn = MT("vn", [128, 4, 512], BF16)
                    st6 = MT("st6", [128, 4, 6]); mv4 = [MT("mv4_%d" % i, [128, 2]) for i in range(4)]
                    rs4 = [MT("rs4_%d" % i, [128, 1]) for i in range(4)]
                    qkpre = [MT("qkpre%d" % c, [128, 3 + HT]) for c in range(2)]
                    cv = [MT("cv%d" % i, [128, HT]) for i in range(2)]
                    qT = MT("qT", [128, 2, HT], BF16)
                    kT = MT("kT", [128, 2, HT])
                    kwT = MT("kwT", [128, 4, HT], BF16)
                    kwtok = MT("kwtok", [128, 4, 256], BF16)
                    vext = MT("vext", [128, 4, 4, 129], BF16)
                    og = MT("og", [128, 4, 512])
                    osig = MT("osig", [128, 512])
                    A1 = MT("A1", [4, HT]); A2 = MT("A2", [4, HT]); A3 = MT("A3", [4, HT])
                    gmax = MT("gmax", [4, 4]); mu = MT("mu", [4, 4]); mh = MT("mh", [4, 5]); dd = MT("dd", [4, 4])
                    dcol = MT("dcol", [128, 2, 4]); wcl = MT("wcl", [128, 4, 8])
                    Sbf = [MT("Sbf%d" % i, [128, 4, 128], BF16) for i in range(2)]
                    Cdbf = [[MT("Cdbf%d_%d" % (i, pc), [128, 258], BF16) for pc in range(2)] for i in range(2)]
                    dmax = MT("dmax", [128, 4]); rden = MT("rden", [128, 4])
                    hbuf = MT("hbuf", [128, 4, 128]); ybuf = vtmp[0]
                    sgt = hbuf
                    yTa = MT("yTa", [128, 4, HT], BF16)
                    yTb = [MT("yTb%d" % i, [128, 4, 128], BF16) for i in range(4)]
                    st6b = MT("st6b", [128, 2, 6]); mvb = MT("mvb", [128, 2]); rsb = MT("rsb", [128, 1])
                    P.op("dve", lambda e: e.memset(vext.ap[:, :, :, 128:129], 1.0), writes=[vext])

                    deferred_ln = []

                    def stepLN1(b):
                        if "ln" in ABL.split(","):
                            return
                        ln_rows("dve", xs[b].ap, 2, 512, mvb, st6b, rsb, [xs[b]], None, "ln1")
                        P.op("dve", lambda e: e.scalar_tensor_tensor(xs[b].ap, xs[b].ap, mvb.ap[:, 0:1], G1.ap, ALU.subtract, ALU.mult),
                             reads=[xs[b], mvb, G1], writes=[xs[b]])
                        P.op("dve", lambda e: e.scalar_tensor_tensor(xs[b].ap, xs[b].ap, rsb.ap[:, 0:1], B1.ap, ALU.mult, ALU.add),
                             reads=[xs[b], rsb, B1], writes=[xs[b]])

                    def stepLN1_batch(blocks):
                        for i, b in enumerate(blocks):
                            for c2 in range(2):
                                P.op("dve", lambda e, b=b, c2=c2: e.bn_stats(st6b.ap[:, c2, :], xs[b].ap[:, c2 * 512:(c2 + 1) * 512]), reads=[xs[b]], writes=[st6b])
                            P.op("dve", lambda e, i=i: e.bn_aggr(mvB.ap[:, i, :], st6b.ap.rearrange("p a b -> p (a b)")), reads=[st6b], writes=[mvB])
                        n = len(blocks)
                        P.op("act", lambda e: e.activation(rsB.ap[:, 0:n], mvB.ap[:, 0:n, 1], AF.Sqrt, bias=epst.ap[:, 0:1]), reads=[mvB, epst], writes=[rsB])
                        P.op("dve", lambda e: e.reciprocal(rsB.ap[:, 0:n], rsB.ap[:, 0:n]), reads=[rsB], writes=[rsB])
                        for i, b in enumerate(blocks):
                            P.op("dve", lambda e, b=b, i=i: e.scalar_tensor_tensor(xs[b].ap, xs[b].ap, mvB.ap[:, i, 0:1], G1.ap, ALU.subtract, ALU.mult),
                                 reads=[xs[b], mvB, G1], writes=[xs[b]])
                            P.op("dve", lambda e, b=b, i=i: e.scalar_tensor_tensor(xs[b].ap, xs[b].ap, rsB.ap[:, i:i + 1], B1.ap, ALU.mult, ALU.add),
                                 reads=[xs[b], rsB, B1], writes=[xs[b]])

                    for h2 in range(2):
                        b0 = 4 * h2
                        for blk in range(4):
                            for kq in range(2):
                                bk = bank()

                                def tr(e, blk=blk, kq=kq, bk=bk, b0=b0):
                                    for i in range(4):
                                        kc = 4 * kq + i
                                        ins = e.transpose(bk.ap[:, i * 128:(i + 1) * 128], xs[b0 + blk].ap[:, kc * 128:(kc + 1) * 128], ident.ap)
                                    return ins
                                P.op("pe", tr, reads=[xs[b0 + blk], ident], writes=[bk])
                                eng = "act" if (blk + kq) % 2 == 0 else "dve"
                                dst = lambda blk=blk, kq=kq: xT.ap[:, 4 * kq:4 * kq + 4, blk * 128:(blk + 1) * 128]
                                src = lambda bk=bk: bk.ap.rearrange("p (a b) -> p a b", a=4)
                                if eng == "act":
                                    P.op("act", lambda e, dst=dst, src=src: e.copy(dst(), src()), reads=[bk], writes=[xT])
                                else:
                                    P.op("dve", lambda e, dst=dst, src=src: e.tensor_copy(dst(), src()), reads=[bk], writes=[xT])

                        def mm_feat(panel, c0, m, bk, nout=HT):
                            def f(e):
                                for kc in range(8):
                                    ins = e.matmul(bk.ap[0:m, 0:nout], panel.ap[:, kc, c0:c0 + m] if panel is not wg else wg.ap[:, kc, c0:c0 + m],
                                                   xT.ap[:, kc, :], start=(kc == 0), stop=(kc == 7))
                                return ins
                            return f

                        def mm_tok(panelv, blk, bk):
                            def f(e):
                                for kc in range(8):
                                    ins = e.matmul(bk.ap, xT.ap[:, kc, blk * 128:(blk + 1) * 128], panelv[:, kc, :],
                                                   start=(kc == 0), stop=(kc == 7))
                                return ins
                            return f

                        bki = bank(); bkf = bank()
                        P.op("pe", mm_feat(wg, 0, 4, bki), reads=[wg, xT], writes=[bki])
                        P.op("pe", mm_feat(wg, 4, 4, bkf), reads=[wg, xT], writes=[bkf])
                        P.op("act", lambda e, bki=bki: e.activation(A1.ap, bki.ap[0:4, :], AF.Identity, bias=gbt.ap[:, 0:1]), reads=[bki, gbt], writes=[A1])
                        P.op("act", lambda e, bkf=bkf: e.activation(A2.ap, bkf.ap[0:4, :], AF.Exp, bias=nbf.ap[:, 0:1], scale=-1.0), reads=[bkf, nbf], writes=[A2])
                        P.op("act", lambda e: e.activation(A2.ap, A2.ap, AF.Ln, bias=1.0), reads=[A2], writes=[A2])
                        P.op("dve", lambda e: e.tensor_tensor_scan(A3.ap, scanm.ap, A2.ap, 0.0, ALU.mult, ALU.add), reads=[scanm, A2], writes=[A3])
                        P.op("dve", lambda e: e.tensor_tensor(A1.ap, A1.ap, A3.ap, ALU.add), reads=[A1, A3], writes=[A1])
                        P.op("dve", lambda e: e.tensor_reduce(gmax.ap, A1.ap.rearrange("p (c t) -> p c t", t=128), AX.X, ALU.max), reads=[A1], writes=[gmax])
                        P.op("dve", lambda e: e.tensor_copy(mh.ap[:, 0:1], mst[l].ap), reads=[mst[l]], writes=[mh])
                        for c in range(4):
                            P.op("dve", lambda e, c=c: e.tensor_tensor(mu.ap[:, c:c + 1], mh.ap[:, c:c + 1], gmax.ap[:, c:c + 1], ALU.max),
                                 reads=[mh, gmax], writes=[mu])
                            P.op("dve", lambda e, c=c: e.tensor_tensor(mh.ap[:, c + 1:c + 2], mu.ap[:, c:c + 1], A3.ap[:, c * 128 + 127:c * 128 + 128], ALU.subtract),
                                 reads=[mu, A3], writes=[mh])
                        P.op("dve", lambda e: e.tensor_copy(mst[l].ap, mh.ap[:, 4:5]), reads=[mh], writes=[mst[l]])
                        P.op("dve", lambda e: e.tensor_tensor(dd.ap, mh.ap[:, 0:4], mu.ap, ALU.subtract), reads=[mh, mu], writes=[dd])
                        P.op("act", lambda e: e.activation(dd.ap, dd.ap, AF.Exp), reads=[dd], writes=[dd])
                        mub = lambda: mu.ap.unsqueeze(2).to_broadcast([4, 4, 128])
                        v3 = lambda t: t.ap.rearrange("p (c t) -> p c t", t=128)
                        P.op("dve", lambda e: e.tensor_tensor(v3(A2), v3(A1), mub(), ALU.subtract), reads=[A1, mu], writes=[A2])
                        P.op("act", lambda e: e.activation(A2.ap, A2.ap, AF.Exp, bias=-LN8), reads=[A2], writes=[A2])
                        P.op("dve", lambda e: e.tensor_tensor(v3(A3), v3(A3), mub(), ALU.subtract), reads=[A3, mu], writes=[A3])
                        P.op("act", lambda e: e.activation(A3.ap, A3.ap, AF.Exp), reads=[A3], writes=[A3])
                        pv = lambda t: t.ap[:, 0:4096].rearrange("p (kc n) -> p kc n", kc=8)
                        pin = ring.get(("in", 0))
                        pinv = Tile.view(pin, "v", pv(pin))
                        for hh in range(4):
                            bk = bank()
                            P.op("pe", mm_feat(pinv, hh * 128, 128, bk), reads=[pin, xT], writes=[bk])
                            P.op("act", lambda e, hh=hh, bk=bk: e.activation(uT.ap[:, hh, :], bk.ap, AF.Gelu), reads=[bk], writes=[uT])
                        if h2 == 1:
                            ring.release(("in", 0))
                        pin = ring.get(("in", 1))
                        for blk in range(4):
                            bk = bank()
                            vt = vtmp[blk]
                            P.op("pe", mm_tok(pv(pin), blk, bk), reads=[pin, xT], writes=[bk])
                            P.op("act", lambda e, bk=bk, vt=vt: e.activation(vt.ap, bk.ap, AF.Gelu), reads=[bk], writes=[vt])
                            for hh in range(4):
                                P.op("dve", lambda e, hh=hh, vt=vt: e.bn_stats(st6.ap[:, hh, :], vt.ap[:, hh * 128:(hh + 1) * 128]),
                                     reads=[vt], writes=[st6])
                            for hh in range(4):
                                P.op("dve", lambda e, hh=hh, blk=blk: e.bn_aggr(vmv[blk].ap[:, hh, :], st6.ap[:, hh, :]), reads=[st6], writes=[vmv[blk]])

                        def v_ln_finish(blk):
                            vt = vtmp[blk]
                            for hh in range(4):
                                P.op("dve", lambda e, hh=hh: e.tensor_scalar(
                                    vn.ap[:, blk, hh * 128:(hh + 1) * 128], vt.ap[:, hh * 128:(hh + 1) * 128],
                                    vmv[blk].ap[:, hh, 0:1], vrs[blk].ap[:, hh:hh + 1], ALU.subtract, ALU.mult),
                                    reads=[vt, vmv[blk], vrs[blk]], writes=[vn])
                        if h2 == 1:
                            ring.release(("in", 1))
                        pin = ring.get(("in", 2))
                        pinv = Tile.view(pin, "v", pv(pin))
                        for c in range(4):
                            bk = bank()
                            qp = qkpre[c % 2]
                            P.op("pe", mm_feat(pinv, c * 128, 128, bk), reads=[pin, xT], writes=[bk])
                            P.op("act", lambda e, c=c, qp=qp: e.copy(qp.ap[:, 0:3], qkh[l][c].ap), reads=[qkh[l][c]], writes=[qp])
                            P.op("act", lambda e, qp=qp, bk=bk: e.copy(qp.ap[:, 3:3 + HT], bk.ap), reads=[bk], writes=[qp])
                            P.op("act", lambda e, c=c, qp=qp: e.copy(qkh[l][c].ap, qp.ap[:, HT:HT + 3]), reads=[qp], writes=[qkh[l][c]])
                            cvt = cv[c % 2]
                            P.op("dve", lambda e, c=c, qp=qp, cvt=cvt: e.tensor_scalar(
                                cvt.ap, qp.ap[:, 0:HT], PPt.ap[:, 4 * c:4 * c + 1], PPt.ap[:, 16 + c:17 + c], ALU.mult, ALU.add),
                                reads=[qp, PPt], writes=[cvt])
                            for tap in range(1, 4):
                                P.op("dve", lambda e, c=c, qp=qp, cvt=cvt, tap=tap: e.scalar_tensor_tensor(
                                    cvt.ap, qp.ap[:, tap:tap + HT], PPt.ap[:, 4 * c + tap:4 * c + tap + 1], cvt.ap, ALU.mult, ALU.add),
                                    reads=[qp, PPt, cvt], writes=[cvt])
                            if c < 2:
                                P.op("act", lambda e, c=c, cvt=cvt: e.activation(qT.ap[:, c, :], cvt.ap, AF.Silu), reads=[cvt], writes=[qT])
                            else:
                                P.op("act", lambda e, c=c, cvt=cvt: e.activation(kT.ap[:, c - 2, :], cvt.ap, AF.Silu), reads=[cvt], writes=[kT])
                        if h2 == 1:
                            ring.release(("in", 2))
                        pin = ring.get(("in", 3))
                        for blk in range(4):
                            bk = bank()
                            P.op("pe", mm_tok(pv(pin), blk, bk), reads=[pin, xT], writes=[bk])
                            P.op("dve", lambda e, blk=blk, bk=bk: e.tensor_copy(vext.ap[:, blk, :, 0:128], bk.ap.rearrange("p (h c) -> p h c", h=4)),
                                 reads=[bk], writes=[vext])
                        if h2 == 1:
                            ring.release(("in", 3))
                        pin = ring.get(("in", 4))
                        for blk in range(4):
                            bk = bank()
                            P.op("pe", mm_tok(pv(pin), blk, bk), reads=[pin, xT], writes=[bk])
                            P.op("act", lambda e, bk=bk: e.activation(osig.ap, bk.ap, AF.Sigmoid), reads=[bk], writes=[osig])
                            P.op("dve", lambda e, blk=blk: e.tensor_tensor(og.ap[:, blk, :], osig.ap, MHG.ap, ALU.mult),
                                 reads=[osig, MHG], writes=[og])
                        if h2 == 1:
                            ring.release(("in", 4))
                        P.op("act", lambda e: e.activation(vrsA.ap, vmvA.ap[:, :, :, 1], AF.Sqrt, bias=epst.ap[:, 0:1]), reads=[vmvA, epst], writes=[vrsA])
                        P.op("dve", lambda e: e.reciprocal(vrsA.ap, vrsA.ap), reads=[vrsA], writes=[vrsA])
                        for blk in range(4):
                            v_ln_finish(blk)
                        if h2 == 1:
                            stepLN1_batch(list(deferred_ln))
                            deferred_ln.clear()
                        for hh in range(4):
                            bk = bank()
                            P.op("pe", lambda e, hh=hh, bk=bk: e.matmul(bk.ap, selh.ap[:, hh, :], A2.ap, start=True, stop=True), reads=[selh, A2], writes=[bk])
                            P.op("dve", lambda e, hh=hh, bk=bk: e.tensor_tensor(kwT.ap[:, hh, :], kT.ap[:, hh // 2, :], bk.ap, ALU.mult), reads=[kT, bk], writes=[kwT])
                        bk = bank()

                        def dsel(e, bk=bk):
                            for pc in range(2):
                                ins = e.matmul(bk.ap[:, 4 * pc:4 * pc + 4], sel.ap[:, pc, :], dd.ap, start=True, stop=True)
                            return ins
                        P.op("pe", dsel, reads=[sel, dd], writes=[bk])
                        P.op("dve", lambda e, bk=bk: e.tensor_copy(dcol.ap.rearrange("p a b -> p (a b)"), bk.ap[:, 0:8]), reads=[bk], writes=[dcol])
                        bk = bank()

                        def wtr(e, bk=bk):
                            for blk in range(4):
                                e.transpose(bk.ap[:, blk * 8:blk * 8 + 4], A2.ap[:, blk * 128:(blk + 1) * 128], ident.ap[0:4, 0:4])
                                ins = e.transpose(bk.ap[:, blk * 8 + 4:blk * 8 + 8], A3.ap[:, blk * 128:(blk + 1) * 128], ident.ap[0:4, 0:4])
                            return ins
                        P.op("pe", wtr, reads=[A2, A3, ident], writes=[bk])
                        P.op("dve", lambda e, bk=bk: e.tensor_copy(wcl.ap.rearrange("p a b -> p (a b)"), bk.ap[:, 0:32]), reads=[bk], writes=[wcl])
                        for blk in range(4):
                            bk = bank()

                            def ktr(e, blk=blk, bk=bk):
                                for pc in range(2):
                                    ins = e.transpose(bk.ap[:, pc * 128:(pc + 1) * 128], kT.ap[:, pc, blk * 128:(blk + 1) * 128], ident.ap)
                                return ins
                            P.op("pe", ktr, reads=[kT, ident], writes=[bk])
                            P.op("dve", lambda e, blk=blk, bk=bk: e.tensor_tensor(
                                kwtok.ap[:, blk, :].rearrange("p (h k) -> p h k", h=4), bk.ap[:, 0:256].rearrange("p (h k) -> p h k", h=4),
                                wcl.ap[:, blk, 0:4].unsqueeze(2).to_broadcast([128, 4, 64]), ALU.mult), reads=[bk, wcl], writes=[kwtok])
                        for hh in range(4):
                            bk = bank()

                            def sgu(e, hh=hh, bk=bk):
                                for blk in range(4):
                                    ins = e.matmul(bk.ap[:, blk * 128:(blk + 1) * 128], vn.ap[:, blk, hh * 128:(hh + 1) * 128], wsT.ap[:, hh, :], start=True, stop=True)
                                return ins
                            P.op("pe", sgu, reads=[vn, wsT], writes=[bk])
                            P.op("dve", lambda e, hh=hh, bk=bk: e.scalar_tensor_tensor(
                                sgt.ap, bk.ap.rearrange("p (a b) -> p a b", a=4), PPt.ap[:, 108 + hh:109 + hh],
                                Kh.ap[:, hh, :].unsqueeze(1).to_broadcast([128, 4, 128]), ALU.mult, ALU.add), reads=[bk, PPt, Kh], writes=[sgt])
                            P.op("dve", lambda e, hh=hh: e.tensor_tensor(yTa.ap[:, hh, :], sgt.ap.rearrange("p a b -> p (a b)"), uT.ap[:, hh, :], ALU.mult),
                                 reads=[sgt, uT], writes=[yTa])
                        po = [ring.get(("out", 0)), ring.get(("out", 1))]
                        chunk_banks = {}

                        def stepA(blk):
                            tk = slice(blk * 128, (blk + 1) * 128)
                            sb = Sbf[blk % 2]
                            cdb = Cdbf[blk % 2]
                            bks = bank()

                            def scores(e):
                                for hh in range(4):
                                    ins = e.matmul(bks.ap[:, hh * 128:(hh + 1) * 128], kwT.ap[:, hh, tk], qT.ap[:, hh // 2, tk], start=True, stop=True)
                                return ins
                            P.op("pe", scores, reads=[kwT, qT], writes=[bks])
                            P.op("dve", lambda e: e.tensor_tensor(
                                sb.ap, bks.ap.rearrange("p (h t) -> p h t", h=4), cmask.ap.unsqueeze(1).to_broadcast([128, 4, 128]), ALU.mult),
                                reads=[bks, cmask], writes=[sb])
                            bku = [bank(), bank()]
                            bkn = [bank(hold=True), bank(hold=True)]
                            chunk_banks[blk] = bkn
                            for pc in range(2):
                                Cp = Cst[l][pc]
                                P.op("dve", lambda e, pc=pc, Cp=Cp: e.tensor_scalar(Cp.ap, Cp.ap, dcol.ap[:, pc, blk:blk + 1], None, ALU.mult),
                                     reads=[Cp, dcol], writes=[Cp])
                                P.op("dve", lambda e, pc=pc, Cp=Cp: e.tensor_tensor(cdb[pc].ap, Cp.ap, bdmask.ap, ALU.mult), reads=[Cp, bdmask], writes=[cdb[pc]])
                                P.op("pe", lambda e, pc=pc: e.matmul(
                                    bku[pc].ap[:, 0:258], kwtok.ap[:, blk, pc * 128:(pc + 1) * 128],
                                    vext.ap[:, blk, 2 * pc:2 * pc + 2, :].rearrange("p a b -> p (a b)"), start=True, stop=True),
                                    reads=[kwtok, vext], writes=[bku[pc]])

                                def num(e, pc=pc):
                                    for hq in range(2):
                                        hh = 2 * pc + hq
                                        o = bkn[pc].ap[:, hq * 129:(hq + 1) * 129]
                                        e.matmul(o, sb.ap[:, hh, :], vext.ap[:, blk, hh, :], start=True, stop=False)
                                        ins = e.matmul(o, qT.ap[:, pc, tk], cdb[pc].ap[:, hq * 129:(hq + 1) * 129], start=False, stop=True)
                                    return ins
                                P.op("pe", num, reads=[sb, vext, qT, cdb[pc]], writes=[bkn[pc]])
                                P.op("dve", lambda e, pc=pc, Cp=Cp: e.tensor_tensor(Cp.ap, Cp.ap, bku[pc].ap[:, 0:258], ALU.add),
                                     reads=[Cp, bku[pc]], writes=[Cp])

                        def stepB(blk):
                            bkn = chunk_banks[blk]
                            for pc in range(2):
                                nv = lambda pc=pc: bkn[pc].ap[:, 0:258].rearrange("p (a b) -> p a b", a=2)
                                P.op("act", lambda e, pc=pc, nv=nv: e.activation(
                                    dmax.ap[:, 2 * pc:2 * pc + 2].unsqueeze(2), nv()[:, :, 128:129], AF.Abs),
                                    reads=[bkn[pc]], writes=[dmax])
                                P.op("dve", lambda e, pc=pc: e.tensor_tensor(
                                    dmax.ap[:, 2 * pc:2 * pc + 2], dmax.ap[:, 2 * pc:2 * pc + 2], wcl.ap[:, blk, 4 + 2 * pc:6 + 2 * pc], ALU.max),
                                    reads=[dmax, wcl], writes=[dmax])
                                P.op("dve", lambda e, pc=pc: e.reciprocal(rden.ap[:, 2 * pc:2 * pc + 2], dmax.ap[:, 2 * pc:2 * pc + 2]), reads=[dmax], writes=[rden])
                                P.op("dve", lambda e, pc=pc, nv=nv: e.tensor_tensor(
                                    hbuf.ap[:, 2 * pc:2 * pc + 2, :], nv()[:, :, 0:128], rden.ap[:, 2 * pc:2 * pc + 2].unsqueeze(2).to_broadcast([128, 2, 128]), ALU.mult),
                                    reads=[bkn[pc], rden], writes=[hbuf])
                            for hh in range(4):
                                if "bln" in ABL:
                                    break
                                P.op("dve", lambda e, hh=hh: e.bn_stats(st6.ap[:, hh, :], hbuf.ap[:, hh, :]), reads=[hbuf], writes=[st6])
                            bm = vmv[blk]; br = vrs[blk]
                            for hh in range(4):
                                P.op("dve", lambda e, hh=hh: e.bn_aggr(bm.ap[:, hh, :], st6.ap[:, hh, :]), reads=[st6], writes=[bm])
                            for hh in range(4):
                                P.op("dve", lambda e, hh=hh: e.scalar_tensor_tensor(
                                    ybuf.ap[:, hh * 128:(hh + 1) * 128], hbuf.ap[:, hh, :], bm.ap[:, hh, 0:1], og.ap[:, blk, hh * 128:(hh + 1) * 128],
                                    ALU.subtract, ALU.mult), reads=[hbuf, bm, og], writes=[ybuf])
                            P.op("act", lambda e: e.activation(br.ap, bm.ap[:, :, 1], AF.Sqrt, bias=epst.ap[:, 0:1]), reads=[bm, epst], writes=[br])
                            P.op("dve", lambda e: e.reciprocal(br.ap, br.ap), reads=[br], writes=[br])
                            for hh in range(4):
                                P.op("act", lambda e, hh=hh: e.mul(ybuf.ap[:, hh * 128:(hh + 1) * 128], ybuf.ap[:, hh * 128:(hh + 1) * 128], br.ap[:, hh:hh + 1]),
                                     reads=[ybuf, br], writes=[ybuf])
                            bk = bank()

                            def ytr(e):
                                for hh in range(4):
                                    ins = e.transpose(bk.ap[:, hh * 128:(hh + 1) * 128], ybuf.ap[:, hh * 128:(hh + 1) * 128], ident.ap)
                                return ins
                            P.op("pe", ytr, reads=[ybuf, ident], writes=[bk])
                            P.op("act", lambda e: e.copy(yTb[blk].ap, bk.ap.rearrange("p (a b) -> p a b", a=4)), reads=[bk], writes=[yTb[blk]])
                            unhold(bkn[0]); unhold(bkn[1])

                        def stepW(blk, b, po, defer=False):
                            for half in range(2):
                                bk = bank()

                                def wo(e, half=half, bk=bk):
                                    for kc in range(8):
                                        lhs = yTa.ap[:, kc, blk * 128:(blk + 1) * 128] if kc < 4 else yTb[blk].ap[:, kc - 4, :]
                                        ins = e.matmul(bk.ap, lhs, pv(po[half])[:, kc, :], start=(kc == 0), stop=(kc == 7))
                                    return ins
                                P.op("pe", wo, reads=[yTa, yTb[blk], po[half]], writes=[bk])
                                xsl = lambda half=half: xs[b].ap[:, half * 512:(half + 1) * 512]
                                P.op("dve", lambda e, xsl=xsl, bk=bk: e.scalar_tensor_tensor(xsl(), xsl(), ALPHA, bk.ap, ALU.mult, ALU.add),
                                     reads=[xs[b], bk], writes=[xs[b]])
                            if defer:
                                deferred_ln.append(b)
                            else:
                                stepLN1(b)

                        if "chunk" not in ABL:
                            stepA(0)
                        for blk in range(4):
                            if "chunk" not in ABL:
                                if blk + 1 < 4:
                                    stepA(blk + 1)
                                stepB(blk)
                            stepW(blk, b0 + blk, po, defer=(h2 == 0))
                    ring.release(("out", 0)); ring.release(("out", 1))
                    prev_events = collect(mt)

                with ExitStack() as fes:
                    inh = prev_events
                    ft = []

                    def FT(name, shape, dtype=F32):
                        t = P.sbuf(sid + "f_" + name, shape, dtype, es=fes, inherit=inh)
                        ft.append(t)
                        return t
                    G2 = FT("G2", [128, D]); B2 = FT("B2", [128, D])
                    P.dma("sp", G2, None, lambda l=l: (G2.ap, lnv_d[l, 2, :].partition_broadcast(128)))
                    P.dma("sp", B2, None, lambda l=l: (B2.ap, lnv_d[l, 3, :].partition_broadcast(128)))
                    x1T = FT("x1T", [128, 8, ST], BF16)
                    hT = FT("hT", [128, NJ, ST], BF16)
                    gpre = [FT("gpre%d" % i, [128, 2 + ST]) for i in range(2)]
                    cvf = [FT("cvf%d" % i, [128, ST]) for i in range(2)]
                    glu = FT("glu", [128, ST])
                    st6c = FT("st6c", [128, 2, 6]); mvc = FT("mvc", [128, 2]); rsc = FT("rsc", [128, 1])
                    for b in range(NB):
                        for kq in range(2):
                            bk = bank()

                            def tr(e, b=b, kq=kq, bk=bk):
                                for i in range(4):
                                    kc = 4 * kq + i
                                    ins = e.transpose(bk.ap[:, i * 128:(i + 1) * 128], xs[b].ap[:, kc * 128:(kc + 1) * 128], ident.ap)
                                return ins
                            P.op("pe", tr, reads=[xs[b], ident], writes=[bk])
                            dst = lambda b=b, kq=kq: x1T.ap[:, 4 * kq:4 * kq + 4, b * 128:(b + 1) * 128]
                            src = lambda bk=bk: bk.ap.rearrange("p (a b) -> p a b", a=4)
                            if (b + kq) % 2 == 0:
                                P.op("act", lambda e, dst=dst, src=src: e.copy(dst(), src()), reads=[bk], writes=[x1T])
                            else:
                                P.op("dve", lambda e, dst=dst, src=src: e.tensor_copy(dst(), src()), reads=[bk], writes=[x1T])
                    for J in range(6):
                        nn = 4 if J < 5 else 2
                        pg = ring.get(("upg", J)); pu = ring.get(("upu", J))
                        ncol = 512 if J < 5 else 256
                        pgv = pg.ap[:, 0:8 * ncol].rearrange("p (kc n) -> p kc n", kc=8)
                        puv = pu.ap[:, 0:8 * ncol].rearrange("p (kc n) -> p kc n", kc=8)
                        for jj in range(nn):
                            j = 4 * J + jj
                            gp = gpre[j % 2]; cvt = cvf[j % 2]
                            bg = [bank(), bank()]
                            bu = [bank(), bank()]
                            for half in range(2):
                                def upm(e, wv, jj=jj, half=half, bk=None):
                                    for kc in range(8):
                                        ins = e.matmul(bk.ap, wv[:, kc, jj * 128:(jj + 1) * 128], x1T.ap[:, kc, half * 512:(half + 1) * 512],
                                                       start=(kc == 0), stop=(kc == 7))
                                    return ins
                                P.op("pe", lambda e, half=half, bk=bg[half], jj=jj, pgv=pgv, upm=upm: upm(e, pgv, jj, half, bk), reads=[pg, x1T], writes=[bg[half]])
                            P.op("act", lambda e, gp=gp, j=j: e.copy(gp.ap[:, 0:2], ghalo[l].ap[:, j, :]), reads=[ghalo[l]], writes=[gp])
                            for half in range(2):
                                P.op("act", lambda e, gp=gp, half=half, bk=bg[half]: e.copy(gp.ap[:, 2 + half * 512:2 + (half + 1) * 512], bk.ap),
                                     reads=[bg[half]], writes=[gp])
                                P.op("act", lambda e, cvt=cvt, half=half, bk=bg[half], j=j: e.activation(
                                    cvt.ap[:, half * 512:(half + 1) * 512], bk.ap, AF.Identity, bias=PPt.ap[:, 86 + j:87 + j], scale=PPt.ap[:, 20 + 3 * j + 2:20 + 3 * j + 3]),
                                    reads=[bg[half], PPt], writes=[cvt])
                            P.op("act", lambda e, gp=gp, j=j: e.copy(ghalo[l].ap[:, j, :], gp.ap[:, ST:ST + 2]), reads=[gp], writes=[ghalo[l]])
                            for tap in range(2):
                                P.op("dve", lambda e, gp=gp, cvt=cvt, tap=tap, j=j: e.scalar_tensor_tensor(
                                    cvt.ap, gp.ap[:, tap:tap + ST], PPt.ap[:, 20 + 3 * j + tap:20 + 3 * j + tap + 1], cvt.ap, ALU.mult, ALU.add),
                                    reads=[gp, PPt, cvt], writes=[cvt])
                            P.op("act", lambda e, cvt=cvt: e.activation(glu.ap, cvt.ap, AF.Gelu), reads=[cvt], writes=[glu])
                            for half in range(2):
                                P.op("pe", lambda e, half=half, bk=bu[half], jj=jj, puv=puv, upm=upm: upm(e, puv, jj, half, bk), reads=[pu, x1T], writes=[bu[half]])
                                P.op("dve", lambda e, half=half, bk=bu[half], j=j: e.tensor_tensor(
                                    hT.ap[:, j, half * 512:(half + 1) * 512], glu.ap[:, half * 512:(half + 1) * 512], bk.ap, ALU.mult),
                                    reads=[glu, bu[half]], writes=[hT])
                        ring.release(("upg", J)); ring.release(("upu", J))
                    pd = [ring.get(("dn", G)) for G in range(6)]
                    for b in range(NB):
                        for half in range(2):
                            bk = bank()

                            def dn(e, b=b, half=half, bk=bk):
                                for j in range(NJ):
                                    G = j // 4
                                    nj = 4 if G < 5 else 2
                                    wv = pd[G].ap[:, 0:nj * 1024].rearrange("p (jc n) -> p jc n", jc=nj)
                                    ins = e.matmul(bk.ap, hT.ap[:, j, b * 128:(b + 1) * 128], wv[:, j % 4, half * 512:(half + 1) * 512],
                                                   start=(j == 0), stop=(j == NJ - 1))
                                return ins
                            P.op("pe", dn, reads=[hT] + pd, writes=[bk])
                            xsl = lambda b=b, half=half: xs[b].ap[:, half * 512:(half + 1) * 512]
                            P.op("dve", lambda e, xsl=xsl, bk=bk: e.scalar_tensor_tensor(xsl(), xsl(), ALPHA, bk.ap, ALU.mult, ALU.add),
                                 reads=[xs[b], bk], writes=[xs[b]])
                        xb = lambda b=b: xs[b].ap
                        if "ln" in ABL.split(","):
                            continue
                        ln_rows("dve", xs[b].ap, 2, 512, mvc, st6c, rsc, [xs[b]], None, "ln2")
                        P.op("dve", lambda e, xb=xb: e.scalar_tensor_tensor(xb(), xb(), mvc.ap[:, 0:1], G2.ap, ALU.subtract, ALU.mult),
                             reads=[xs[b], mvc, G2], writes=[xs[b]])
                        P.op("dve", lambda e, xb=xb: e.scalar_tensor_tensor(xb(), xb(), rsc.ap[:, 0:1], B2.ap, ALU.mult, ALU.add),
                             reads=[xs[b], rsc, B2], writes=[xs[b]])
                        if l == L - 1:
                            P.dma("sp", None, xs[b], lambda st=st, b=b: (y_d[st * ST + b * 128:st * ST + (b + 1) * 128, :], xs[b].ap), out_dram=True)
                            if st + 1 < nst:
                                P.dma("sp", xs[b], None, lambda st=st, b=b: (xs[b].ap, x_d[(st + 1) * ST + b * 128:(st + 1) * ST + (b + 1) * 128, :]))
                    for G in range(6):
                        ring.release(("dn", G))
                    prev_events = collect(ft)
        for st in range(nst):
            for l in range(L):
                stage(st, l)
        P.finish()
    return nc, P


N_CORES = 4
_CACHE = {}


def prep_inputs(inp):
    f = lambda a: np.ascontiguousarray(np.asarray(a, dtype=np.float32))
    pp = np.zeros((L, 128, NPP), np.float32)
    qcw = f(inp["qk_conv_w"]); qcb = f(inp["qk_conv_b"]); fcw = f(inp["ffn_conv_w"]); fcb = f(inp["ffn_conv_b"])
    slg = f(inp["sgu_ln_g"]); slb = f(inp["sgu_ln_b"])
    for l in range(L):
        pp[l, :, 0:16] = qcw[l].reshape(4, 4, 128).transpose(2, 1, 0).reshape(128, 16)
        pp[l, :, 16:20] = qcb[l].reshape(4, 128).T
        pp[l, :, 20:86] = fcw[l].reshape(3, NJ, 128).transpose(2, 1, 0).reshape(128, 66)
        pp[l, :, 86:108] = fcb[l].reshape(NJ, 128).T
        pp[l, :, 108:112] = slg[l].T
        pp[l, :, 112:116] = slb[l].T
    sguT = np.ascontiguousarray(f(inp["sgu_w"]).transpose(0, 3, 1, 2))
    sgub = f(inp["sgu_b"]).reshape(L, 512)
    lnv = np.zeros((L, 5, D), np.float32)
    lnv[:, 0] = f(inp["ln1_g"]); lnv[:, 1] = f(inp["ln1_b"]); lnv[:, 2] = f(inp["ln2_g"]); lnv[:, 3] = f(inp["ln2_b"])
    lnv[:, 4, 0:512] = f(inp["mh_norm_g"]).reshape(L, 512)
    gb = np.stack([f(inp["b_igate"]), f(inp["b_fgate"])], axis=-1)
    return dict(w_in=f(inp["w_in"]), w_out=f(inp["w_out"]), w_up=f(inp["ffn_w_up"]), w_dn=f(inp["ffn_w_down"]),
                pp=pp, sguT=sguT, sgub=sgub, lnv=lnv, gb=np.ascontiguousarray(gb))


def kernel(**inputs):
    x = np.asarray(inputs["x"], dtype=np.float32)
    bsz, seq, _ = x.shape
    shared = prep_inputs(inputs)
    key = seq
    if key not in _CACHE:
        _CACHE[key] = build_program(seq)[0]
    nc = _CACHE[key]
    in_maps = []
    for b in range(bsz):
        m = dict(shared)
        m["x"] = np.ascontiguousarray(x[b])
        in_maps.append(m)
    res = run_bass_kernel_spmd(nc, in_maps, core_ids=list(range(bsz)))
    return np.stack([np.asarray(r["y"], dtype=np.float32) for r in res.results], axis=0)
```

```python
import numpy as np
import concourse.bass as bass
import concourse.mybir as mybir
from concourse.bass_utils import run_bass_kernel_spmd
from contextlib import ExitStack

import os
LIMIT = int(os.environ.get("KLIMIT", "1000000000"))
ABL = os.environ.get("KABL", "")
F32 = mybir.dt.float32
BF16 = mybir.dt.bfloat16
AF = mybir.ActivationFunctionType
ALU = mybir.AluOpType
AX = mybir.AxisListType


class Tile:
    def __init__(self, name, ap):
        self.name = name
        self.ap = ap
        self.w = None
        self.r = []

    def view(self, name, ap):
        t = Tile.__new__(Tile)
        t.name = name
        t.ap = ap
        t._parent = self
        return t


def _root(t):
    while hasattr(t, "_parent"):
        t = t._parent
    return t


class Prog:
    ENG = ("pe", "act", "dve", "pool", "sp")
    NDMA = 12

    def __init__(self, nc, es):
        self.nc = nc
        self.es = es
        self.items = {e: [] for e in self.ENG}
        self.sems = {}
        self.count = {}
        for e in ("pe", "act", "dve", "pool"):
            self.sems[e] = es.enter_context(nc.semaphore("s_" + e))
            self.count[e] = 0
        self.dma_sems = {}
        self.dma_rr = {}
        for q in ("sp", "pool", "act"):
            for i in range(self.NDMA):
                k = "d_%s_%d" % (q, i)
                self.sems[k] = es.enter_context(nc.semaphore(k))
                self.count[k] = 0
            self.dma_rr[q] = 0
        self.waited = {e: {} for e in self.ENG}
        self.out_events = []
        self.n_ops = 0
        self.role_ap = None

    def sbuf(self, name, shape, dtype, es=None, inherit=None):
        t = (es or self.es).enter_context(self.nc.sbuf_tensor(name, list(shape), dtype))
        tl = Tile(name, t[:])
        if inherit:
            tl.r = list(inherit)
        return tl

    def psum(self, name, shape, dtype):
        t = self.es.enter_context(self.nc.psum_tensor(name, list(shape), dtype))
        return Tile(name, t[:])

    def _need(self, eng, ev):
        if ev is None:
            return
        k, v = ev
        if self.waited[eng].get(k, 0) >= v:
            return
        self.waited[eng][k] = v
        self.items[eng].append(("wait", k, v))

    def _deps(self, eng, reads, writes):
        for t in reads:
            t = _root(t)
            self._need(eng, t.w)
        for t in writes:
            t = _root(t)
            self._need(eng, t.w)
            for ev in t.r:
                self._need(eng, ev)

    def _mark(self, ev, reads, writes):
        for t in reads:
            t = _root(t)
            t.r.append(ev)
        for t in writes:
            t = _root(t)
            t.w = ev
            t.r = []

    def op(self, eng, fn, reads=(), writes=()):
        self.n_rec = getattr(self, "n_rec", 0) + 1
        if self.n_rec > LIMIT:
            return None
        self._deps(eng, reads, writes)
        self.count[eng] += 1
        ev = (eng, self.count[eng])
        self.items[eng].append(("op", fn, eng, 1))
        self.waited[eng][eng] = max(self.waited[eng].get(eng, 0), 0)
        self._mark(ev, reads, writes)
        self.n_ops += 1
        return ev

    def dram(self, name, shape, dtype):
        t = self.nc.dram_tensor(name, list(shape), dtype, kind="Internal")
        return Tile(name, t.ap())

    def collective(self, send, recv, groups):
        return self.dma("pool", recv, send, None, coll=groups)

    def dma(self, q, dst, src, fn, out_dram=False, cond=None, coll=None):
        self.n_rec = getattr(self, "n_rec", 0) + 1
        if self.n_rec > LIMIT:
            return None
        reads = [src] if src is not None else []
        writes = [dst] if dst is not None else []
        eng = q
        self._deps(eng, reads, writes)
        i = self.dma_rr[q]
        self.dma_rr[q] = (i + 1) % self.NDMA
        k = "d_%s_%d" % (q, i)
        if self.count[k] > 0:
            self._need(eng, (k, self.count[k]))
        self.count[k] += 16
        ev = (k, self.count[k])
        if coll is not None:
            self.items[eng].append(("op", lambda e, dst=dst, src=src, coll=coll: e.collective_compute(
                "AllGather", ALU.bypass, replica_groups=coll, ins=[src.ap], outs=[dst.ap]), k, 16))
        elif cond is None:
            self.items[eng].append(("op", lambda e, fn=fn: e.dma_start(out=fn()[0], in_=fn()[1]), k, 16))
        else:
            self.items[eng].append(("op", lambda e, fn=fn, cond=cond: e.dma_start(out=fn()[0], in_=fn()[1], cond=cond(self.rt[e])), k, 16))
        self._mark(ev, reads, writes)
        if out_dram:
            self.out_events.append(ev)
        return ev

    def make_identity(self, ident, eng="pool"):
        n = ident.ap.shape[0]
        self.op("pool", lambda e: e.memset(ident.ap, 0.0), writes=[ident])
        self.op("pool", lambda e: e.affine_select(
            out=ident.ap, in_=ident.ap, pattern=[[-1, n]], compare_op=ALU.not_equal,
            fill=1.0, base=0, channel_multiplier=1), reads=[ident], writes=[ident])

    def finish(self):
        for ev in self.out_events:
            self._need("sp", ev)
        nc = self.nc
        items = self.items
        sems = self.sems

        self.rt = {}

        def replay(eng_name, e):
            if self.role_ap is not None and eng_name in ("sp", "pool"):
                self.rt[e] = e.value_load(self.role_ap, min_val=0, max_val=1)
            for it in items[eng_name]:
                if it[0] == "wait":
                    e.wait_ge(sems[it[1]], it[2])
                else:
                    ins = it[1](e)
                    ins.then_inc(sems[it[2]], it[3])

        with nc.Block() as block:
            @block.tensor
            def _(e):
                replay("pe", e)

            @block.scalar
            def _(e):
                replay("act", e)

            @block.vector
            def _(e):
                replay("dve", e)

            @block.gpsimd
            def _(e):
                replay("pool", e)

            @block.sync
            def _(e):
                replay("sp", e)


L = 2
D = 1024
DIN = 2568
DFF = 2816
NJ = 22
ST = 1024
NB = ST // 128
HT = 512
ALPHA = float((2 * L) ** 0.25)
LN_EPS = 1e-5
LN8 = float(np.log(8.0))
NPP = 116
NRING = 8


class WRing:
    def __init__(self, P, n):
        self.P = P
        self.n = n
        self.slots = [P.sbuf("wr%d" % i, [128, 4096], BF16) for i in range(n)]
        self.busy = [False] * n
        self.pending = []
        self.loaded = {}
        self.seq = 0

    def request(self, key, mk):
        self.pending.append((key, mk))

    def pump(self):
        while self.pending:
            s = self.seq % self.n
            if self.busy[s]:
                return
            key, mk = self.pending.pop(0)
            slot = self.slots[s]
            self.P.dma("pool", slot, None, lambda mk=mk, slot=slot: mk(slot.ap))
            self.busy[s] = True
            self.loaded[key] = s
            self.seq += 1

    def get(self, key):
        self.pump()
        assert key in self.loaded, key
        return self.slots[self.loaded[key]]

    def release(self, key):
        s = self.loaded.pop(key)
        self.busy[s] = False
        self.pump()


def build_program(n_tok):
    nst = n_tok // ST
    nc = bass.Bass("TRN2", target_bir_lowering=False)
    dt = lambda name, shape, kind="ExternalInput": nc.dram_tensor(name, list(shape), F32, kind=kind).ap()
    x_d = dt("x", [n_tok, D])
    w_in_d = dt("w_in", [L, D, DIN])
    w_out_d = dt("w_out", [L, D, D])
    w_up_d = dt("w_up", [L, D, 2 * DFF])
    w_dn_d = dt("w_dn", [L, DFF, D])
    pp_d = dt("pp", [L, 128, NPP])
    sguT_d = dt("sguT", [L, 128, 4, 128])
    sgub_d = dt("sgub", [L, 512])
    lnv_d = dt("lnv", [L, 5, D])
    gb_d = dt("gb", [L, 4, 2])
    y_d = dt("y", [n_tok, D], kind="ExternalOutput")

    with ExitStack() as es:
        P = Prog(nc, es)
        S = lambda name, shape, dtype=F32: P.sbuf(name, shape, dtype)
        ident = S("ident", [128, 128])
        cmask = S("cmask", [128, 128])
        ones128 = S("ones128", [128, 128])
        sel = S("sel", [4, 2, 128])
        selh = S("selh", [4, 4, 128])
        bdmask = S("bdmask", [128, 258])
        scanm = S("scanm", [4, HT])
        epst = S("epst", [128, 1])
        xs = [S("xs%d" % b, [128, D]) for b in range(NB)]
        ring = WRing(P, NRING)
        wg = S("wg", [128, 8, 8], BF16)
        PPt = S("PPt", [128, NPP])
        MHG = S("MHG", [128, 512])
        gbt = S("gbt", [4, 2])
        nbf = S("nbf", [4, 1])
        wsT = S("wsT", [128, 4, 128], BF16)
        Kh = S("Kh", [128, 4, 128])
        Cst = [[S("C%d_%d" % (l, pc), [128, 258]) for pc in range(2)] for l in range(L)]
        mst = [S("m%d" % l, [4, 1]) for l in range(L)]
        qkh = [[S("qkh%d_%d" % (l, c), [128, 3]) for c in range(4)] for l in range(L)]
        ghalo = [S("ghalo%d" % l, [128, NJ, 2]) for l in range(L)]
        banks = [P.psum("bank%d" % i, [128, 512], F32) for i in range(8)]
        bstate = {"i": 0}

        held = set()

        def bank(hold=False):
            while (bstate["i"] % 8) in held:
                bstate["i"] += 1
            k = bstate["i"] % 8
            bstate["i"] += 1
            if hold:
                held.add(k)
            return banks[k]

        def unhold(b):
            held.discard(banks.index(b))

        P.make_identity(ident)
        P.op("pool", lambda e: e.memset(cmask.ap, 1.0), writes=[cmask])
        P.op("pool", lambda e: e.affine_select(out=cmask.ap, in_=cmask.ap, pattern=[[1, 128]],
                                               compare_op=ALU.is_ge, fill=0.0, base=0, channel_multiplier=-1),
             reads=[cmask], writes=[cmask])
        P.op("pool", lambda e: e.memset(ones128.ap, 1.0), writes=[ones128])
        P.op("pool", lambda e: e.memset(sel.ap, 1.0), writes=[sel])
        for pc in range(2):
            P.op("pool", lambda e, pc=pc: e.affine_select(out=sel.ap[:, pc, :], in_=sel.ap[:, pc, :], pattern=[[1, 128]],
                                                          compare_op=ALU.is_ge, fill=0.0, base=128 * pc, channel_multiplier=-64),
                 reads=[sel], writes=[sel])
            P.op("pool", lambda e, pc=pc: e.affine_select(out=sel.ap[:, pc, :], in_=sel.ap[:, pc, :], pattern=[[-1, 128]],
                                                          compare_op=ALU.is_ge, fill=0.0, base=63 - 128 * pc, channel_multiplier=64),
                 reads=[sel], writes=[sel])
        for hh in range(4):
            P.op("pool", lambda e, hh=hh: e.tensor_copy(selh.ap[:, hh, :], sel.ap[:, hh // 2, :]), reads=[sel], writes=[selh])
            z0 = 64 if hh % 2 == 0 else 0
            P.op("pool", lambda e, hh=hh, z0=z0: e.memset(selh.ap[:, hh, z0:z0 + 64], 0.0), writes=[selh])
        P.op("pool", lambda e: e.memset(bdmask.ap, 0.0), writes=[bdmask])
        P.op("pool", lambda e: e.memset(bdmask.ap[0:64, 0:129], 1.0), writes=[bdmask])
        P.op("pool", lambda e: e.memset(bdmask.ap[64:128, 129:258], 1.0), writes=[bdmask])
        P.op("pool", lambda e: e.memset(scanm.ap, 1.0), writes=[scanm])
        P.op("pool", lambda e: e.memset(epst.ap, LN_EPS), writes=[epst])
        P.op("pool", lambda e: e.memset(scanm.ap.rearrange("p (c t) -> p c t", t=128)[:, :, 0:1], 0.0), writes=[scanm])
        for l in range(L):
            for pc in range(2):
                P.op("pool", lambda e, t=Cst[l][pc]: e.memset(t.ap, 0.0), writes=[Cst[l][pc]])
            P.op("pool", lambda e, t=mst[l]: e.memset(t.ap, 0.0), writes=[mst[l]])
            for c in range(4):
                P.op("pool", lambda e, t=qkh[l][c]: e.memset(t.ap, 0.0), writes=[qkh[l][c]])
            P.op("pool", lambda e, t=ghalo[l]: e.memset(t.ap, 0.0), writes=[ghalo[l]])

        def kview(w_d, c0, n):
            return w_d.rearrange("(kc p) n -> p kc n", p=128)[:, :, c0:c0 + n]

        def request_stage_weights(l):
            for p in range(5):
                ring.request(("in", p), lambda slot, p=p, l=l: (
                    slot[:, 0:4096].rearrange("p (kc n) -> p kc n", kc=8), kview(w_in_d[l], 512 * p, 512)))
            for h in range(2):
                ring.request(("out", h), lambda slot, h=h, l=l: (
                    slot[:, 0:4096].rearrange("p (kc n) -> p kc n", kc=8), kview(w_out_d[l], 512 * h, 512)))
            for J in range(6):
                n = 512 if J < 5 else 256
                ring.request(("upg", J), lambda slot, J=J, n=n, l=l: (
                    slot[:, 0:8 * n].rearrange("p (kc n) -> p kc n", kc=8), kview(w_up_d[l], 512 * J, n)))
                ring.request(("upu", J), lambda slot, J=J, n=n, l=l: (
                    slot[:, 0:8 * n].rearrange("p (kc n) -> p kc n", kc=8), kview(w_up_d[l], DFF + 512 * J, n)))
            for G in range(6):
                nj = 4 if G < 5 else 2
                ring.request(("dn", G), lambda slot, G=G, nj=nj, l=l: (
                    slot[:, 0:nj * 1024].rearrange("p (jc n) -> p jc n", jc=nj),
                    w_dn_d[l].rearrange("(jc p) n -> p jc n", p=128)[:, 4 * G:4 * G + nj, :]))

        prev_events = []

        def collect(tiles):
            evs = []
            for t in tiles:
                if t.w is not None:
                    evs.append(t.w)
                evs.extend(t.r)
            return list(dict.fromkeys(evs))

        def ln_rows(eng_stats, src_ap, nchunk, width, mv, st6, rs, tiles_r, tiles_w, name):
            for i in range(nchunk):
                P.op("dve", lambda e, i=i: e.bn_stats(st6.ap[:, i, :], src_ap[:, i * width:(i + 1) * width]),
                     reads=tiles_r, writes=[st6])
            P.op("dve", lambda e: e.bn_aggr(mv.ap, st6.ap[:, 0:nchunk, :].rearrange("p a b -> p (a b)")),
                 reads=[st6], writes=[mv])
            P.op("act", lambda e: e.activation(rs.ap, mv.ap[:, 1:2], AF.Sqrt, bias=epst.ap[:, 0:1]), reads=[mv, epst], writes=[rs])
            P.op("dve", lambda e: e.reciprocal(rs.ap, rs.ap), reads=[rs], writes=[rs])

        def stage(st, l):
            nonlocal prev_events
            if True:
                sid = "s%d_%d" % (st, l)
                request_stage_weights(l)
                ring.pump()
                P.dma("sp", PPt, None, lambda l=l: (PPt.ap, pp_d[l]))
                P.dma("sp", MHG, None, lambda l=l: (MHG.ap, lnv_d[l, 4, 0:512].partition_broadcast(128)))
                P.dma("sp", gbt, None, lambda l=l: (gbt.ap, gb_d[l]))
                P.dma("pool", wg, None, lambda l=l: (wg.ap, kview(w_in_d[l], 2560, 8)))
                P.dma("sp", Kh, None, lambda l=l: (Kh.ap.rearrange("p h i -> p (h i)"), sgub_d[l].partition_broadcast(128)))
                P.op("dve", lambda e: e.tensor_scalar(nbf.ap, gbt.ap[:, 1:2], -1.0, None, ALU.mult), reads=[gbt], writes=[nbf])
                if l == 0 and st == 0:
                    for b in range(NB):
                        P.dma("sp", xs[b], None, lambda st=st, b=b: (xs[b].ap, x_d[st * ST + b * 128:st * ST + (b + 1) * 128, :]))

                with ExitStack() as mes:
                    inh = prev_events
                    M = lambda name, shape, dtype=F32: P.sbuf(sid + name, shape, dtype, es=mes, inherit=inh)
                    mt = []

                    def MT(name, shape, dtype=F32):
                        t = M(name, shape, dtype)
                        mt.append(t)
                        return t
                    G1 = MT("G1", [128, D]); B1 = MT("B1", [128, D])
                    P.dma("sp", G1, None, lambda l=l: (G1.ap, lnv_d[l, 0, :].partition_broadcast(128)))
                    P.dma("sp", B1, None, lambda l=l: (B1.ap, lnv_d[l, 1, :].partition_broadcast(128)))
                    wsT32 = MT("wsT32", [128, 4, 128])
                    P.dma("sp", wsT32, None, lambda l=l: (wsT32.ap, sguT_d[l]))
                    P.op("dve", lambda e: e.memset(wsT32.ap[64:128, :, 0:64], 0.0), writes=[wsT32])
                    P.op("act", lambda e: e.copy(wsT.ap, wsT32.ap), reads=[wsT32], writes=[wsT])
                    bk = bank()
                    P.op("pe", lambda e, bk=bk: e.matmul(bk.ap, ones128.ap, wsT32.ap.rearrange("p h i -> p (h i)"), start=True, stop=True),
                         reads=[ones128, wsT32], writes=[bk])
                    for hh in range(4):
                        P.op("dve", lambda e, hh=hh, bk=bk: e.scalar_tensor_tensor(
                            Kh.ap[:, hh, :], bk.ap[:, hh * 128:(hh + 1) * 128], PPt.ap[:, 112 + hh:113 + hh], Kh.ap[:, hh, :],
                            ALU.mult, ALU.add), reads=[bk, PPt, Kh], writes=[Kh])

                    xT = MT("xT", [128, 8, HT], BF16)
                    uT = MT("uT", [128, 4, HT])
                    vtmp = [MT("vtmp%d" % i, [128, 512]) for i in range(4)]
                    vmvA = MT("vmvA", [128, 4, 4, 2]); vrsA = MT("vrsA", [128, 4, 4])
                    vmv = [Tile.view(vmvA, "vmv%d" % i, vmvA.ap[:, i]) for i in range(4)]
                    vrs = [Tile.view(vrsA, "vrs%d" % i, vrsA.ap[:, i]) for i in range(4)]
                    mvB = MT("mvB", [128, 4, 2]); rsB = MT("rsB", [128, 4])
                    vn = MT("vn", [128, 4, 512], BF16)
                    st6 = MT("st6", [128, 4, 6]); mv4 = [MT("mv4_%d" % i, [128, 2]) for i in range(4)]
                    rs4 = [MT("rs4_%d" % i, [128, 1]) for i in range(4)]
                    qkpre = [MT("qkpre%d" % c, [128, 3 + HT]) for c in range(2)]
                    cv = [MT("cv%d" % i, [128, HT]) for i in range(2)]
                    qT = MT("qT", [128, 2, HT], BF16)
                    kT = MT("kT", [128, 2, HT])
                    kwT = MT("kwT", [128, 4, HT], BF16)
                    kwtok = MT("kwtok", [128, 4, 256], BF16)
                    vext = MT("vext", [128, 4, 4, 129], BF16)
                    og = MT("og", [128, 4, 512])
                    osig = MT("osig", [128, 512])
                    A1 = MT("A1", [4, HT]); A2 = MT("A2", [4, HT]); A3 = MT("A3", [4, HT])
                    gmax = MT("gmax", [4, 4]); mu = MT("mu", [4, 4]); mh = MT("mh", [4, 5]); dd = MT("dd", [4, 4])
                    dcol = MT("dcol", [128, 2, 4]); wcl = MT("wcl", [128, 4, 8])
                    Sbf = [MT("Sbf%d" % i, [128, 4, 128], BF16) for i in range(2)]
                    Cdbf = [[MT("Cdbf%d_%d" % (i, pc), [128, 258], BF16) for pc in range(2)] for i in range(2)]
                    dmax = MT("dmax", [128, 4]); rden = MT("rden", [128, 4])
                    hbuf = MT("hbuf", [128, 4, 128]); ybuf = vtmp[0]
                    sgt = hbuf
                    yTa = MT("yTa", [128, 4, HT], BF16)
                    yTb = [MT("yTb%d" % i, [128, 4, 128], BF16) for i in range(4)]
                    st6b = MT("st6b", [128, 2, 6]); mvb = MT("mvb", [128, 2]); rsb = MT("rsb", [128, 1])
                    P.op("dve", lambda e: e.memset(vext.ap[:, :, :, 128:129], 1.0), writes=[vext])

                    deferred_ln = []

                    def stepLN1(b):
                        if "ln" in ABL.split(","):
                            return
                        ln_rows("dve", xs[b].ap, 2, 512, mvb, st6b, rsb, [xs[b]], None, "ln1")
                        P.op("dve", lambda e: e.scalar_tensor_tensor(xs[b].ap, xs[b].ap, mvb.ap[:, 0:1], G1.ap, ALU.subtract, ALU.mult),
                             reads=[xs[b], mvb, G1], writes=[xs[b]])
                        P.op("dve", lambda e: e.scalar_tensor_tensor(xs[b].ap, xs[b].ap, rsb.ap[:, 0:1], B1.ap, ALU.mult, ALU.add),
                             reads=[xs[b], rsb, B1], writes=[xs[b]])

                    def stepLN1_batch(blocks):
                        for i, b in enumerate(blocks):
                            for c2 in range(2):
                                P.op("dve", lambda e, b=b, c2=c2: e.bn_stats(st6b.ap[:, c2, :], xs[b].ap[:, c2 * 512:(c2 + 1) * 512]), reads=[xs[b]], writes=[st6b])
                            P.op("dve", lambda e, i=i: e.bn_aggr(mvB.ap[:, i, :], st6b.ap.rearrange("p a b -> p (a b)")), reads=[st6b], writes=[mvB])
                        n = len(blocks)
                        P.op("act", lambda e: e.activation(rsB.ap[:, 0:n], mvB.ap[:, 0:n, 1], AF.Sqrt, bias=epst.ap[:, 0:1]), reads=[mvB, epst], writes=[rsB])
                        P.op("dve", lambda e: e.reciprocal(rsB.ap[:, 0:n], rsB.ap[:, 0:n]), reads=[rsB], writes=[rsB])
                        for i, b in enumerate(blocks):
                            P.op("dve", lambda e, b=b, i=i: e.scalar_tensor_tensor(xs[b].ap, xs[b].ap, mvB.ap[:, i, 0:1], G1.ap, ALU.subtract, ALU.mult),
                                 reads=[xs[b], mvB, G1], writes=[xs[b]])
                            P.op("dve", lambda e, b=b, i=i: e.scalar_tensor_tensor(xs[b].ap, xs[b].ap, rsB.ap[:, i:i + 1], B1.ap, ALU.mult, ALU.add),
                                 reads=[xs[b], rsB, B1], writes=[xs[b]])

                    for h2 in range(2):
                        b0 = 4 * h2
                        for blk in range(4):
                            for kq in range(2):
                                bk = bank()

                                def tr(e, blk=blk, kq=kq, bk=bk, b0=b0):
                                    for i in range(4):
                                        kc = 4 * kq + i
                                        ins = e.transpose(bk.ap[:, i * 128:(i + 1) * 128], xs[b0 + blk].ap[:, kc * 128:(kc + 1) * 128], ident.ap)
                                    return ins
                                P.op("pe", tr, reads=[xs[b0 + blk], ident], writes=[bk])
                                eng = "act" if (blk + kq) % 2 == 0 else "dve"
                                dst = lambda blk=blk, kq=kq: xT.ap[:, 4 * kq:4 * kq + 4, blk * 128:(blk + 1) * 128]
                                src = lambda bk=bk: bk.ap.rearrange("p (a b) -> p a b", a=4)
                                if eng == "act":
                                    P.op("act", lambda e, dst=dst, src=src: e.copy(dst(), src()), reads=[bk], writes=[xT])
                                else:
                                    P.op("dve", lambda e, dst=dst, src=src: e.tensor_copy(dst(), src()), reads=[bk], writes=[xT])

                        def mm_feat(panel, c0, m, bk, nout=HT):
                            def f(e):
                                for kc in range(8):
                                    ins = e.matmul(bk.ap[0:m, 0:nout], panel.ap[:, kc, c0:c0 + m] if panel is not wg else wg.ap[:, kc, c0:c0 + m],
                                                   xT.ap[:, kc, :], start=(kc == 0), stop=(kc == 7))
                                return ins
                            return f

                        def mm_tok(panelv, blk, bk):
                            def f(e):
                                for kc in range(8):
                                    ins = e.matmul(bk.ap, xT.ap[:, kc, blk * 128:(blk + 1) * 128], panelv[:, kc, :],
                                                   start=(kc == 0), stop=(kc == 7))
                                return ins
                            return f

                        bki = bank(); bkf = bank()
                        P.op("pe", mm_feat(wg, 0, 4, bki), reads=[wg, xT], writes=[bki])
                        P.op("pe", mm_feat(wg, 4, 4, bkf), reads=[wg, xT], writes=[bkf])
                        P.op("act", lambda e, bki=bki: e.activation(A1.ap, bki.ap[0:4, :], AF.Identity, bias=gbt.ap[:, 0:1]), reads=[bki, gbt], writes=[A1])
                        P.op("act", lambda e, bkf=bkf: e.activation(A2.ap, bkf.ap[0:4, :], AF.Exp, bias=nbf.ap[:, 0:1], scale=-1.0), reads=[bkf, nbf], writes=[A2])
                        P.op("act", lambda e: e.activation(A2.ap, A2.ap, AF.Ln, bias=1.0), reads=[A2], writes=[A2])
                        P.op("dve", lambda e: e.tensor_tensor_scan(A3.ap, scanm.ap, A2.ap, 0.0, ALU.mult, ALU.add), reads=[scanm, A2], writes=[A3])
                        P.op("dve", lambda e: e.tensor_tensor(A1.ap, A1.ap, A3.ap, ALU.add), reads=[A1, A3], writes=[A1])
                        P.op("dve", lambda e: e.tensor_reduce(gmax.ap, A1.ap.rearrange("p (c t) -> p c t", t=128), AX.X, ALU.max), reads=[A1], writes=[gmax])
                        P.op("dve", lambda e: e.tensor_copy(mh.ap[:, 0:1], mst[l].ap), reads=[mst[l]], writes=[mh])
                        for c in range(4):
                            P.op("dve", lambda e, c=c: e.tensor_tensor(mu.ap[:, c:c + 1], mh.ap[:, c:c + 1], gmax.ap[:, c:c + 1], ALU.max),
                                 reads=[mh, gmax], writes=[mu])
                            P.op("dve", lambda e, c=c: e.tensor_tensor(mh.ap[:, c + 1:c + 2], mu.ap[:, c:c + 1], A3.ap[:, c * 128 + 127:c * 128 + 128], ALU.subtract),
                                 reads=[mu, A3], writes=[mh])
                        P.op("dve", lambda e: e.tensor_copy(mst[l].ap, mh.ap[:, 4:5]), reads=[mh], writes=[mst[l]])
                        P.op("dve", lambda e: e.tensor_tensor(dd.ap, mh.ap[:, 0:4], mu.ap, ALU.subtract), reads=[mh, mu], writes=[dd])
                        P.op("act", lambda e: e.activation(dd.ap, dd.ap, AF.Exp), reads=[dd], writes=[dd])
                        mub = lambda: mu.ap.unsqueeze(2).to_broadcast([4, 4, 128])
                        v3 = lambda t: t.ap.rearrange("p (c t) -> p c t", t=128)
                        P.op("dve", lambda e: e.tensor_tensor(v3(A2), v3(A1), mub(), ALU.subtract), reads=[A1, mu], writes=[A2])
                        P.op("act", lambda e: e.activation(A2.ap, A2.ap, AF.Exp, bias=-LN8), reads=[A2], writes=[A2])
                        P.op("dve", lambda e: e.tensor_tensor(v3(A3), v3(A3), mub(), ALU.subtract), reads=[A3, mu], writes=[A3])
                        P.op("act", lambda e: e.activation(A3.ap, A3.ap, AF.Exp), reads=[A3], writes=[A3])
                        pv = lambda t: t.ap[:, 0:4096].rearrange("p (kc n) -> p kc n", kc=8)
                        pin = ring.get(("in", 0))
                        pinv = Tile.view(pin, "v", pv(pin))
                        for hh in range(4):
                            bk = bank()
                            P.op("pe", mm_feat(pinv, hh * 128, 128, bk), reads=[pin, xT], writes=[bk])
                            P.op("act", lambda e, hh=hh, bk=bk: e.activation(uT.ap[:, hh, :], bk.ap, AF.Gelu), reads=[bk], writes=[uT])
                        if h2 == 1:
                            ring.release(("in", 0))
                        pin = ring.get(("in", 1))
                        for blk in range(4):
                            bk = bank()
                            vt = vtmp[blk]
                            P.op("pe", mm_tok(pv(pin), blk, bk), reads=[pin, xT], writes=[bk])
                            P.op("act", lambda e, bk=bk, vt=vt: e.activation(vt.ap, bk.ap, AF.Gelu), reads=[bk], writes=[vt])
                            for hh in range(4):
                                P.op("dve", lambda e, hh=hh, vt=vt: e.bn_stats(st6.ap[:, hh, :], vt.ap[:, hh * 128:(hh + 1) * 128]),
                                     reads=[vt], writes=[st6])
                            for hh in range(4):
                                P.op("dve", lambda e, hh=hh, blk=blk: e.bn_aggr(vmv[blk].ap[:, hh, :], st6.ap[:, hh, :]), reads=[st6], writes=[vmv[blk]])

                        def v_ln_finish(blk):
                            vt = vtmp[blk]
                            for hh in range(4):
                                P.op("dve", lambda e, hh=hh: e.tensor_scalar(
                                    vn.ap[:, blk, hh * 128:(hh + 1) * 128], vt.ap[:, hh * 128:(hh + 1) * 128],
                                    vmv[blk].ap[:, hh, 0:1], vrs[blk].ap[:, hh:hh + 1], ALU.subtract, ALU.mult),
                                    reads=[vt, vmv[blk], vrs[blk]], writes=[vn])
                        if h2 == 1:
                            ring.release(("in", 1))
                        pin = ring.get(("in", 2))
                        pinv = Tile.view(pin, "v", pv(pin))
                        for c in range(4):
                            bk = bank()
                            qp = qkpre[c % 2]
                            P.op("pe", mm_feat(pinv, c * 128, 128, bk), reads=[pin, xT], writes=[bk])
                            P.op("act", lambda e, c=c, qp=qp: e.copy(qp.ap[:, 0:3], qkh[l][c].ap), reads=[qkh[l][c]], writes=[qp])
                            P.op("act", lambda e, qp=qp, bk=bk: e.copy(qp.ap[:, 3:3 + HT], bk.ap), reads=[bk], writes=[qp])
                            P.op("act", lambda e, c=c, qp=qp: e.copy(qkh[l][c].ap, qp.ap[:, HT:HT + 3]), reads=[qp], writes=[qkh[l][c]])
                            cvt = cv[c % 2]
                            P.op("dve", lambda e, c=c, qp=qp, cvt=cvt: e.tensor_scalar(
                                cvt.ap, qp.ap[:, 0:HT], PPt.ap[:, 4 * c:4 * c + 1], PPt.ap[:, 16 + c:17 + c], ALU.mult, ALU.add),
                                reads=[qp, PPt], writes=[cvt])
                            for tap in range(1, 4):
                                P.op("dve", lambda e, c=c, qp=qp, cvt=cvt, tap=tap: e.scalar_tensor_tensor(
                                    cvt.ap, qp.ap[:, tap:tap + HT], PPt.ap[:, 4 * c + tap:4 * c + tap + 1], cvt.ap, ALU.mult, ALU.add),
                                    reads=[qp, PPt, cvt], writes=[cvt])
                            if c < 2:
                                P.op("act", lambda e, c=c, cvt=cvt: e.activation(qT.ap[:, c, :], cvt.ap, AF.Silu), reads=[cvt], writes=[qT])
                            else:
                                P.op("act", lambda e, c=c, cvt=cvt: e.activation(kT.ap[:, c - 2, :], cvt.ap, AF.Silu), reads=[cvt], writes=[kT])
                        if h2 == 1:
                            ring.release(("in", 2))
                        pin = ring.get(("in", 3))
                        for blk in range(4):
                            bk = bank()
                            P.op("pe", mm_tok(pv(pin), blk, bk), reads=[pin, xT], writes=[bk])
                            P.op("dve", lambda e, blk=blk, bk=bk: e.tensor_copy(vext.ap[:, blk, :, 0:128], bk.ap.rearrange("p (h c) -> p h c", h=4)),
                                 reads=[bk], writes=[vext])
                        if h2 == 1:
                            ring.release(("in", 3))
                        pin = ring.get(("in", 4))
                        for blk in range(4):
                            bk = bank()
                            P.op("pe", mm_tok(pv(pin), blk, bk), reads=[pin, xT], writes=[bk])
                            P.op("act", lambda e, bk=bk: e.activation(osig.ap, bk.ap, AF.Sigmoid), reads=[bk], writes=[osig])
                            P.op("dve", lambda e, blk=blk: e.tensor_tensor(og.ap[:, blk, :], osig.ap, MHG.ap, ALU.mult),
                                 reads=[osig, MHG], writes=[og])
                        if h2 == 1:
                            ring.release(("in", 4))
                        P.op("act", lambda e: e.activation(vrsA.ap, vmvA.ap[:, :, :, 1], AF.Sqrt, bias=epst.ap[:, 0:1]), reads=[vmvA, epst], writes=[vrsA])
                        P.op("dve", lambda e: e.reciprocal(vrsA.ap, vrsA.ap), reads=[vrsA], writes=[vrsA])
                        for blk in range(4):
                            v_ln_finish(blk)
                        if h2 == 1:
                            for b in deferred_ln:
                                stepLN1(b)
                            deferred_ln.clear()
                        for hh in range(4):
                            bk = bank()
                            P.op("pe", lambda e, hh=hh, bk=bk: e.matmul(bk.ap, selh.ap[:, hh, :], A2.ap, start=True, stop=True), reads=[selh, A2], writes=[bk])
                            P.op("dve", lambda e, hh=hh, bk=bk: e.tensor_tensor(kwT.ap[:, hh, :], kT.ap[:, hh // 2, :], bk.ap, ALU.mult), reads=[kT, bk], writes=[kwT])
                        bk = bank()

                        def dsel(e, bk=bk):
                            for pc in range(2):
                                ins = e.matmul(bk.ap[:, 4 * pc:4 * pc + 4], sel.ap[:, pc, :], dd.ap, start=True, stop=True)
                            return ins
                        P.op("pe", dsel, reads=[sel, dd], writes=[bk])
                        P.op("dve", lambda e, bk=bk: e.tensor_copy(dcol.ap.rearrange("p a b -> p (a b)"), bk.ap[:, 0:8]), reads=[bk], writes=[dcol])
                        bk = bank()

                        def wtr(e, bk=bk):
                            for blk in range(4):
                                e.transpose(bk.ap[:, blk * 8:blk * 8 + 4], A2.ap[:, blk * 128:(blk + 1) * 128], ident.ap[0:4, 0:4])
                                ins = e.transpose(bk.ap[:, blk * 8 + 4:blk * 8 + 8], A3.ap[:, blk * 128:(blk + 1) * 128], ident.ap[0:4, 0:4])
                            return ins
                        P.op("pe", wtr, reads=[A2, A3, ident], writes=[bk])
                        P.op("dve", lambda e, bk=bk: e.tensor_copy(wcl.ap.rearrange("p a b -> p (a b)"), bk.ap[:, 0:32]), reads=[bk], writes=[wcl])
                        for blk in range(4):
                            bk = bank()

                            def ktr(e, blk=blk, bk=bk):
                                for pc in range(2):
                                    ins = e.transpose(bk.ap[:, pc * 128:(pc + 1) * 128], kT.ap[:, pc, blk * 128:(blk + 1) * 128], ident.ap)
                                return ins
                            P.op("pe", ktr, reads=[kT, ident], writes=[bk])
                            P.op("dve", lambda e, blk=blk, bk=bk: e.tensor_tensor(
                                kwtok.ap[:, blk, :].rearrange("p (h k) -> p h k", h=4), bk.ap[:, 0:256].rearrange("p (h k) -> p h k", h=4),
                                wcl.ap[:, blk, 0:4].unsqueeze(2).to_broadcast([128, 4, 64]), ALU.mult), reads=[bk, wcl], writes=[kwtok])
                        for hh in range(4):
                            bk = bank()

                            def sgu(e, hh=hh, bk=bk):
                                for blk in range(4):
                                    ins = e.matmul(bk.ap[:, blk * 128:(blk + 1) * 128], vn.ap[:, blk, hh * 128:(hh + 1) * 128], wsT.ap[:, hh, :], start=True, stop=True)
                                return ins
                            P.op("pe", sgu, reads=[vn, wsT], writes=[bk])
                            P.op("dve", lambda e, hh=hh, bk=bk: e.scalar_tensor_tensor(
                                sgt.ap, bk.ap.rearrange("p (a b) -> p a b", a=4), PPt.ap[:, 108 + hh:109 + hh],
                                Kh.ap[:, hh, :].unsqueeze(1).to_broadcast([128, 4, 128]), ALU.mult, ALU.add), reads=[bk, PPt, Kh], writes=[sgt])
                            P.op("dve", lambda e, hh=hh: e.tensor_tensor(yTa.ap[:, hh, :], sgt.ap.rearrange("p a b -> p (a b)"), uT.ap[:, hh, :], ALU.mult),
                                 reads=[sgt, uT], writes=[yTa])
                        po = [ring.get(("out", 0)), ring.get(("out", 1))]
                        chunk_banks = {}

                        def stepA(blk):
                            tk = slice(blk * 128, (blk + 1) * 128)
                            sb = Sbf[blk % 2]
                            cdb = Cdbf[blk % 2]
                            bks = bank()

                            def scores(e):
                                for hh in range(4):
                                    ins = e.matmul(bks.ap[:, hh * 128:(hh + 1) * 128], kwT.ap[:, hh, tk], qT.ap[:, hh // 2, tk], start=True, stop=True)
                                return ins
                            P.op("pe", scores, reads=[kwT, qT], writes=[bks])
                            P.op("dve", lambda e: e.tensor_tensor(
                                sb.ap, bks.ap.rearrange("p (h t) -> p h t", h=4), cmask.ap.unsqueeze(1).to_broadcast([128, 4, 128]), ALU.mult),
                                reads=[bks, cmask], writes=[sb])
                            bku = [bank(), bank()]
                            bkn = [bank(hold=True), bank(hold=True)]
                            chunk_banks[blk] = bkn
                            for pc in range(2):
                                Cp = Cst[l][pc]
                                P.op("dve", lambda e, pc=pc, Cp=Cp: e.tensor_scalar(Cp.ap, Cp.ap, dcol.ap[:, pc, blk:blk + 1], None, ALU.mult),
                                     reads=[Cp, dcol], writes=[Cp])
                                P.op("dve", lambda e, pc=pc, Cp=Cp: e.tensor_tensor(cdb[pc].ap, Cp.ap, bdmask.ap, ALU.mult), reads=[Cp, bdmask], writes=[cdb[pc]])
                                P.op("pe", lambda e, pc=pc: e.matmul(
                                    bku[pc].ap[:, 0:258], kwtok.ap[:, blk, pc * 128:(pc + 1) * 128],
                                    vext.ap[:, blk, 2 * pc:2 * pc + 2, :].rearrange("p a b -> p (a b)"), start=True, stop=True),
                                    reads=[kwtok, vext], writes=[bku[pc]])

                                def num(e, pc=pc):
                                    for hq in range(2):
                                        hh = 2 * pc + hq
                                        o = bkn[pc].ap[:, hq * 129:(hq + 1) * 129]
                                        e.matmul(o, sb.ap[:, hh, :], vext.ap[:, blk, hh, :], start=True, stop=False)
                                        ins = e.matmul(o, qT.ap[:, pc, tk], cdb[pc].ap[:, hq * 129:(hq + 1) * 129], start=False, stop=True)
                                    return ins
                                P.op("pe", num, reads=[sb, vext, qT, cdb[pc]], writes=[bkn[pc]])
                                P.op("dve", lambda e, pc=pc, Cp=Cp: e.tensor_tensor(Cp.ap, Cp.ap, bku[pc].ap[:, 0:258], ALU.add),
                                     reads=[Cp, bku[pc]], writes=[Cp])

                        def stepB(blk):
                            bkn = chunk_banks[blk]
                            for pc in range(2):
                                nv = lambda pc=pc: bkn[pc].ap[:, 0:258].rearrange("p (a b) -> p a b", a=2)
                                P.op("act", lambda e, pc=pc, nv=nv: e.activation(
                                    dmax.ap[:, 2 * pc:2 * pc + 2].unsqueeze(2), nv()[:, :, 128:129], AF.Abs),
                                    reads=[bkn[pc]], writes=[dmax])
                                P.op("dve", lambda e, pc=pc: e.tensor_tensor(
                                    dmax.ap[:, 2 * pc:2 * pc + 2], dmax.ap[:, 2 * pc:2 * pc + 2], wcl.ap[:, blk, 4 + 2 * pc:6 + 2 * pc], ALU.max),
                                    reads=[dmax, wcl], writes=[dmax])
                                P.op("dve", lambda e, pc=pc: e.reciprocal(rden.ap[:, 2 * pc:2 * pc + 2], dmax.ap[:, 2 * pc:2 * pc + 2]), reads=[dmax], writes=[rden])
                                P.op("dve", lambda e, pc=pc, nv=nv: e.tensor_tensor(
                                    hbuf.ap[:, 2 * pc:2 * pc + 2, :], nv()[:, :, 0:128], rden.ap[:, 2 * pc:2 * pc + 2].unsqueeze(2).to_broadcast([128, 2, 128]), ALU.mult),
                                    reads=[bkn[pc], rden], writes=[hbuf])
                            for hh in range(4):
                                if "bln" in ABL:
                                    break
                                P.op("dve", lambda e, hh=hh: e.bn_stats(st6.ap[:, hh, :], hbuf.ap[:, hh, :]), reads=[hbuf], writes=[st6])
                            bm = vmv[blk]; br = vrs[blk]
                            for hh in range(4):
                                P.op("dve", lambda e, hh=hh: e.bn_aggr(bm.ap[:, hh, :], st6.ap[:, hh, :]), reads=[st6], writes=[bm])
                            for hh in range(4):
                                P.op("dve", lambda e, hh=hh: e.scalar_tensor_tensor(
                                    ybuf.ap[:, hh * 128:(hh + 1) * 128], hbuf.ap[:, hh, :], bm.ap[:, hh, 0:1], og.ap[:, blk, hh * 128:(hh + 1) * 128],
                                    ALU.subtract, ALU.mult), reads=[hbuf, bm, og], writes=[ybuf])
                            P.op("act", lambda e: e.activation(br.ap, bm.ap[:, :, 1], AF.Sqrt, bias=epst.ap[:, 0:1]), reads=[bm, epst], writes=[br])
                            P.op("dve", lambda e: e.reciprocal(br.ap, br.ap), reads=[br], writes=[br])
                            for hh in range(4):
                                P.op("act", lambda e, hh=hh: e.mul(ybuf.ap[:, hh * 128:(hh + 1) * 128], ybuf.ap[:, hh * 128:(hh + 1) * 128], br.ap[:, hh:hh + 1]),
                                     reads=[ybuf, br], writes=[ybuf])
                            bk = bank()

                            def ytr(e):
                                for hh in range(4):
                                    ins = e.transpose(bk.ap[:, hh * 128:(hh + 1) * 128], ybuf.ap[:, hh * 128:(hh + 1) * 128], ident.ap)
                                return ins
                            P.op("pe", ytr, reads=[ybuf, ident], writes=[bk])
                            P.op("act", lambda e: e.copy(yTb[blk].ap, bk.ap.rearrange("p (a b) -> p a b", a=4)), reads=[bk], writes=[yTb[blk]])
                            unhold(bkn[0]); unhold(bkn[1])

                        def stepW(blk, b, po, defer=False):
                            for half in range(2):
                                bk = bank()

                                def wo(e, half=half, bk=bk):
                                    for kc in range(8):
                                        lhs = yTa.ap[:, kc, blk * 128:(blk + 1) * 128] if kc < 4 else yTb[blk].ap[:, kc - 4, :]
                                        ins = e.matmul(bk.ap, lhs, pv(po[half])[:, kc, :], start=(kc == 0), stop=(kc == 7))
                                    return ins
                                P.op("pe", wo, reads=[yTa, yTb[blk], po[half]], writes=[bk])
                                xsl = lambda half=half: xs[b].ap[:, half * 512:(half + 1) * 512]
                                P.op("dve", lambda e, xsl=xsl, bk=bk: e.scalar_tensor_tensor(xsl(), xsl(), ALPHA, bk.ap, ALU.mult, ALU.add),
                                     reads=[xs[b], bk], writes=[xs[b]])
                            if defer:
                                deferred_ln.append(b)
                            else:
                                stepLN1(b)

                        if "chunk" not in ABL:
                            stepA(0)
                        for blk in range(4):
                            if "chunk" not in ABL:
                                if blk + 1 < 4:
                                    stepA(blk + 1)
                                stepB(blk)
                            stepW(blk, b0 + blk, po, defer=(h2 == 0))
                    ring.release(("out", 0)); ring.release(("out", 1))
                    prev_events = collect(mt)

                with ExitStack() as fes:
                    inh = prev_events
                    ft = []

                    def FT(name, shape, dtype=F32):
                        t = P.sbuf(sid + "f_" + name, shape, dtype, es=fes, inherit=inh)
                        ft.append(t)
                        return t
                    G2 = FT("G2", [128, D]); B2 = FT("B2", [128, D])
                    P.dma("sp", G2, None, lambda l=l: (G2.ap, lnv_d[l, 2, :].partition_broadcast(128)))
                    P.dma("sp", B2, None, lambda l=l: (B2.ap, lnv_d[l, 3, :].partition_broadcast(128)))
                    x1T = FT("x1T", [128, 8, ST], BF16)
                    hT = FT("hT", [128, NJ, ST], BF16)
                    gpre = [FT("gpre%d" % i, [128, 2 + ST]) for i in range(2)]
                    cvf = [FT("cvf%d" % i, [128, ST]) for i in range(2)]
                    glu = FT("glu", [128, ST])
                    st6c = FT("st6c", [128, 2, 6]); mvc = FT("mvc", [128, 2]); rsc = FT("rsc", [128, 1])
                    for b in range(NB):
                        for kq in range(2):
                            bk = bank()

                            def tr(e, b=b, kq=kq, bk=bk):
                                for i in range(4):
                                    kc = 4 * kq + i
                                    ins = e.transpose(bk.ap[:, i * 128:(i + 1) * 128], xs[b].ap[:, kc * 128:(kc + 1) * 128], ident.ap)
                                return ins
                            P.op("pe", tr, reads=[xs[b], ident], writes=[bk])
                            dst = lambda b=b, kq=kq: x1T.ap[:, 4 * kq:4 * kq + 4, b * 128:(b + 1) * 128]
                            src = lambda bk=bk: bk.ap.rearrange("p (a b) -> p a b", a=4)
                            if (b + kq) % 2 == 0:
                                P.op("act", lambda e, dst=dst, src=src: e.copy(dst(), src()), reads=[bk], writes=[x1T])
                            else:
                                P.op("dve", lambda e, dst=dst, src=src: e.tensor_copy(dst(), src()), reads=[bk], writes=[x1T])
                    for J in range(6):
                        nn = 4 if J < 5 else 2
                        pg = ring.get(("upg", J)); pu = ring.get(("upu", J))
                        ncol = 512 if J < 5 else 256
                        pgv = pg.ap[:, 0:8 * ncol].rearrange("p (kc n) -> p kc n", kc=8)
                        puv = pu.ap[:, 0:8 * ncol].rearrange("p (kc n) -> p kc n", kc=8)
                        for jj in range(nn):
                            j = 4 * J + jj
                            gp = gpre[j % 2]; cvt = cvf[j % 2]
                            bg = [bank(), bank()]
                            bu = [bank(), bank()]
                            for half in range(2):
                                def upm(e, wv, jj=jj, half=half, bk=None):
                                    for kc in range(8):
                                        ins = e.matmul(bk.ap, wv[:, kc, jj * 128:(jj + 1) * 128], x1T.ap[:, kc, half * 512:(half + 1) * 512],
                                                       start=(kc == 0), stop=(kc == 7))
                                    return ins
                                P.op("pe", lambda e, half=half, bk=bg[half], jj=jj, pgv=pgv, upm=upm: upm(e, pgv, jj, half, bk), reads=[pg, x1T], writes=[bg[half]])
                            P.op("act", lambda e, gp=gp, j=j: e.copy(gp.ap[:, 0:2], ghalo[l].ap[:, j, :]), reads=[ghalo[l]], writes=[gp])
                            for half in range(2):
                                P.op("act", lambda e, gp=gp, half=half, bk=bg[half]: e.copy(gp.ap[:, 2 + half * 512:2 + (half + 1) * 512], bk.ap),
                                     reads=[bg[half]], writes=[gp])
                                P.op("act", lambda e, cvt=cvt, half=half, bk=bg[half], j=j: e.activation(
                                    cvt.ap[:, half * 512:(half + 1) * 512], bk.ap, AF.Identity, bias=PPt.ap[:, 86 + j:87 + j], scale=PPt.ap[:, 20 + 3 * j + 2:20 + 3 * j + 3]),
                                    reads=[bg[half], PPt], writes=[cvt])
                            P.op("act", lambda e, gp=gp, j=j: e.copy(ghalo[l].ap[:, j, :], gp.ap[:, ST:ST + 2]), reads=[gp], writes=[ghalo[l]])
                            for tap in range(2):
                                P.op("dve", lambda e, gp=gp, cvt=cvt, tap=tap, j=j: e.scalar_tensor_tensor(
                                    cvt.ap, gp.ap[:, tap:tap + ST], PPt.ap[:, 20 + 3 * j + tap:20 + 3 * j + tap + 1], cvt.ap, ALU.mult, ALU.add),
                                    reads=[gp, PPt, cvt], writes=[cvt])
                            P.op("act", lambda e, cvt=cvt: e.activation(glu.ap, cvt.ap, AF.Gelu), reads=[cvt], writes=[glu])
                            for half in range(2):
                                P.op("pe", lambda e, half=half, bk=bu[half], jj=jj, puv=puv, upm=upm: upm(e, puv, jj, half, bk), reads=[pu, x1T], writes=[bu[half]])
                                P.op("dve", lambda e, half=half, bk=bu[half], j=j: e.tensor_tensor(
                                    hT.ap[:, j, half * 512:(half + 1) * 512], glu.ap[:, half * 512:(half + 1) * 512], bk.ap, ALU.mult),
                                    reads=[glu, bu[half]], writes=[hT])
                        ring.release(("upg", J)); ring.release(("upu", J))
                    pd = [ring.get(("dn", G)) for G in range(6)]
                    for b in range(NB):
                        for half in range(2):
                            bk = bank()

                            def dn(e, b=b, half=half, bk=bk):
                                for j in range(NJ):
                                    G = j // 4
                                    nj = 4 if G < 5 else 2
                                    wv = pd[G].ap[:, 0:nj * 1024].rearrange("p (jc n) -> p jc n", jc=nj)
                                    ins = e.matmul(bk.ap, hT.ap[:, j, b * 128:(b + 1) * 128], wv[:, j % 4, half * 512:(half + 1) * 512],
                                                   start=(j == 0), stop=(j == NJ - 1))
                                return ins
                            P.op("pe", dn, reads=[hT] + pd, writes=[bk])
                            xsl = lambda b=b, half=half: xs[b].ap[:, half * 512:(half + 1) * 512]
                            P.op("dve", lambda e, xsl=xsl, bk=bk: e.scalar_tensor_tensor(xsl(), xsl(), ALPHA, bk.ap, ALU.mult, ALU.add),
                                 reads=[xs[b], bk], writes=[xs[b]])
                        xb = lambda b=b: xs[b].ap
                        if "ln" in ABL.split(","):
                            continue
                        ln_rows("dve", xs[b].ap, 2, 512, mvc, st6c, rsc, [xs[b]], None, "ln2")
                        P.op("dve", lambda e, xb=xb: e.scalar_tensor_tensor(xb(), xb(), mvc.ap[:, 0:1], G2.ap, ALU.subtract, ALU.mult),
                             reads=[xs[b], mvc, G2], writes=[xs[b]])
                        P.op("dve", lambda e, xb=xb: e.scalar_tensor_tensor(xb(), xb(), rsc.ap[:, 0:1], B2.ap, ALU.mult, ALU.add),
                             reads=[xs[b], rsc, B2], writes=[xs[b]])
                        if l == L - 1:
                            P.dma("sp", None, xs[b], lambda st=st, b=b: (y_d[st * ST + b * 128:st * ST + (b + 1) * 128, :], xs[b].ap), out_dram=True)
                            if st + 1 < nst:
                                P.dma("sp", xs[b], None, lambda st=st, b=b: (xs[b].ap, x_d[(st + 1) * ST + b * 128:(st + 1) * ST + (b + 1) * 128, :]))
                    for G in range(6):
                        ring.release(("dn", G))
                    prev_events = collect(ft)
        for st in range(nst):
            for l in range(L):
                stage(st, l)
        P.finish()
    return nc, P


N_CORES = 4
_CACHE = {}


def prep_inputs(inp):
    f = lambda a: np.ascontiguousarray(np.asarray(a, dtype=np.float32))
    pp = np.zeros((L, 128, NPP), np.float32)
    qcw = f(inp["qk_conv_w"]); qcb = f(inp["qk_conv_b"]); fcw = f(inp["ffn_conv_w"]); fcb = f(inp["ffn_conv_b"])
    slg = f(inp["sgu_ln_g"]); slb = f(inp["sgu_ln_b"])
    for l in range(L):
        pp[l, :, 0:16] = qcw[l].reshape(4, 4, 128).transpose(2, 1, 0).reshape(128, 16)
        pp[l, :, 16:20] = qcb[l].reshape(4, 128).T
        pp[l, :, 20:86] = fcw[l].reshape(3, NJ, 128).transpose(2, 1, 0).reshape(128, 66)
        pp[l, :, 86:108] = fcb[l].reshape(NJ, 128).T
        pp[l, :, 108:112] = slg[l].T
        pp[l, :, 112:116] = slb[l].T
    sguT = np.ascontiguousarray(f(inp["sgu_w"]).transpose(0, 3, 1, 2))
    sgub = f(inp["sgu_b"]).reshape(L, 512)
    lnv = np.zeros((L, 5, D), np.float32)
    lnv[:, 0] = f(inp["ln1_g"]); lnv[:, 1] = f(inp["ln1_b"]); lnv[:, 2] = f(inp["ln2_g"]); lnv[:, 3] = f(inp["ln2_b"])
    lnv[:, 4, 0:512] = f(inp["mh_norm_g"]).reshape(L, 512)
    gb = np.stack([f(inp["b_igate"]), f(inp["b_fgate"])], axis=-1)
    return dict(w_in=f(inp["w_in"]), w_out=f(inp["w_out"]), w_up=f(inp["ffn_w_up"]), w_dn=f(inp["ffn_w_down"]),
                pp=pp, sguT=sguT, sgub=sgub, lnv=lnv, gb=np.ascontiguousarray(gb))


def kernel(**inputs):
    x = np.asarray(inputs["x"], dtype=np.float32)
    bsz, seq, _ = x.shape
    shared = prep_inputs(inputs)
    key = seq
    if key not in _CACHE:
        _CACHE[key] = build_program(seq)[0]
    nc = _CACHE[key]
    in_maps = []
    for b in range(bsz):
        m = dict(shared)
        m["x"] = np.ascontiguousarray(x[b])
        in_maps.append(m)
    res = run_bass_kernel_spmd(nc, in_maps, core_ids=list(range(bsz)))
    return np.stack([np.asarray(r["y"], dtype=np.float32) for r in res.results], axis=0)
```
